# Optimizing a Trainium2 kernel written in Bass

```python
import math
import jax, jax.numpy as jnp
from jax import lax
import numpy as np

D_MODEL = 1024
BATCH = 16
SEQ = 4096
DEPTH = 1
DEC_BATCH = 8
DEC_SEQ = 32
PAST_LEN = 1024

CHUNK = 64
Q_BLOCK = 128
D_MIX = 2 * D_MODEL
D_SSM = D_MIX // 2
D_ATTN = D_MIX - D_SSM
SSM_HEAD_DIM = 64
N_SSM_HEADS = D_SSM // SSM_HEAD_DIM
N_GROUPS = 2
D_STATE = 128
D_BC = N_GROUPS * D_STATE
CONV_W = 4
D_XBC = D_SSM + 2 * D_BC
ATTN_HEAD_DIM = 64
N_ATTN_HEADS = D_ATTN // ATTN_HEAD_DIM
ATTN_SCALE = ATTN_HEAD_DIM ** -0.5
ALPHA = (2 * DEPTH) ** 0.25
BETA = (8 * DEPTH) ** -0.25
LN_EPS = 1e-5
RMS_EPS = 1e-5
Z_SSM_END = D_SSM
XBC_END = Z_SSM_END + D_XBC
DT_END = XBC_END + N_SSM_HEADS
Q_END = DT_END + D_ATTN
K_END = Q_END + D_ATTN
V_END = K_END + D_ATTN
Z_ATTN_END = V_END + D_ATTN
D_IN_PROJ = Z_ATTN_END + N_ATTN_HEADS

kernel_name = 'hymba_ssd_fox_streaming_step'


def causal_conv(xbc, conv_state, conv_w, conv_b):
    xp = jnp.concatenate([conv_state.astype(xbc.dtype), xbc], axis=1)
    y = lax.conv_general_dilated(xp, conv_w[:, None, :].astype(xbc.dtype), window_strides=(1,),
                                 padding='VALID', dimension_numbers=('NWC', 'WIO', 'NWC'),
                                 feature_group_count=xbc.shape[-1])
    return jax.nn.silu(y + conv_b.astype(xbc.dtype)), xp[:, xp.shape[1] - (CONV_W - 1):]


def ssd_scan(x, dt, a, b_in, c_in, h0, chunk):
    f32 = jnp.float32
    bsz, l = x.shape[0], x.shape[1]
    nc = l // chunk
    e = N_SSM_HEADS // N_GROUPS
    xc = x.astype(f32).reshape(bsz, nc, chunk, N_GROUPS, e, SSM_HEAD_DIM)
    dtc = dt.reshape(bsz, nc, chunk, N_GROUPS, e)
    bm = b_in.astype(f32).reshape(bsz, nc, chunk, N_GROUPS, D_STATE)
    cm = c_in.astype(f32).reshape(bsz, nc, chunk, N_GROUPS, D_STATE)
    cs = jnp.cumsum(dtc * a.reshape(N_GROUPS, e), axis=2)
    xdt = xc * dtc[..., None]
    tril = jnp.tril(jnp.ones((chunk, chunk), dtype=bool))
    diff = cs[:, :, :, None] - cs[:, :, None, :]
    decay = jnp.exp(jnp.where(tril[:, :, None, None], diff, -jnp.inf))
    cb = jnp.einsum('bcqgn,bcsgn->bcqsg', cm, bm)
    y_diag = jnp.einsum('bcqsg,bcqsge,bcsgep->bcqgep', cb, decay, xdt)
    decay_to_end = jnp.exp(cs[:, :, -1:] - cs)
    states = jnp.einsum('bcsgn,bcsge,bcsgep->bcgepn', bm, decay_to_end, xdt)
    chunk_decay = jnp.exp(cs[:, :, -1])
    h_init = h0.astype(f32).reshape(bsz, N_GROUPS, e, SSM_HEAD_DIM, D_STATE)

    def step(h, inp):
        s_c, d_c = inp
        return d_c[..., None, None] * h + s_c, h

    h_last, h_prev = lax.scan(step, h_init, (jnp.moveaxis(states, 1, 0), jnp.moveaxis(chunk_decay, 1, 0)))
    h_prev = jnp.moveaxis(h_prev, 0, 1)
    y_off = jnp.einsum('bcqgn,bcgepn,bcqge->bcqgep', cm, h_prev, jnp.exp(cs))
    y = (y_diag + y_off).reshape(bsz, l, N_SSM_HEADS, SSM_HEAD_DIM)
    return y, h_last.reshape(bsz, N_SSM_HEADS, SSM_HEAD_DIM, D_STATE)


def fox_attend(q, k, v, fq, fk, q_pos, k_pos):
    s = jnp.einsum('bqhd,bkhd->bhqk', q, k, preferred_element_type=jnp.float32) * ATTN_SCALE
    s = s + jnp.swapaxes(fq, 1, 2)[:, :, :, None] - jnp.swapaxes(fk, 1, 2)[:, :, None, :]
    s = jnp.where((k_pos[None, :] <= q_pos[:, None])[None, None], s, -jnp.inf)
    p = jax.nn.softmax(s, axis=-1)
    return jnp.einsum('bhqk,bkhd->bqhd', p.astype(v.dtype), v)


def hybrid_layer(x, conv_state, ssm_state, past_k, past_v, past_logf,
                 w_in, conv_w, conv_b, dt_bias, a_log, d_skip, ssm_norm_w, f_bias,
                 w_out, ln_g, ln_b, ssm_chunk, q_block):
    f32 = jnp.float32
    bsz, t = x.shape[0], x.shape[1]
    past = past_k.shape[1]
    proj = x @ w_in
    z_ssm = proj[..., :Z_SSM_END]
    xbc = proj[..., Z_SSM_END:XBC_END]
    dt_raw = proj[..., XBC_END:DT_END]
    q = proj[..., DT_END:Q_END].reshape(bsz, t, N_ATTN_HEADS, ATTN_HEAD_DIM)
    k = proj[..., Q_END:K_END].reshape(bsz, t, N_ATTN_HEADS, ATTN_HEAD_DIM)
    v = proj[..., K_END:V_END].reshape(bsz, t, N_ATTN_HEADS, ATTN_HEAD_DIM)
    z_attn = proj[..., V_END:Z_ATTN_END]
    f_raw = proj[..., Z_ATTN_END:]
    xbc, new_conv = causal_conv(xbc, conv_state, conv_w, conv_b)
    xs = xbc[..., :D_SSM].reshape(bsz, t, N_SSM_HEADS, SSM_HEAD_DIM)
    b_ssm = xbc[..., D_SSM:D_SSM + D_BC].reshape(bsz, t, N_GROUPS, D_STATE)
    c_ssm = xbc[..., D_SSM + D_BC:].reshape(bsz, t, N_GROUPS, D_STATE)
    dt = jax.nn.softplus(dt_raw.astype(f32) + dt_bias.astype(f32))
    a = -jnp.exp(a_log.astype(f32))
    y_ssm, new_ssm = ssd_scan(xs, dt, a, b_ssm, c_ssm, ssm_state, ssm_chunk)
    y_ssm = y_ssm + d_skip.astype(f32)[:, None] * xs.astype(f32)
    g = (y_ssm.reshape(bsz, t, D_SSM) * jax.nn.silu(z_ssm.astype(f32))).reshape(bsz, t, N_GROUPS, D_SSM // N_GROUPS)
    g = g * lax.rsqrt(jnp.mean(g * g, axis=-1, keepdims=True) + RMS_EPS)
    y_ssm = g.reshape(bsz, t, D_SSM) * ssm_norm_w.astype(f32)
    logf = jax.nn.log_sigmoid(f_raw.astype(f32) + f_bias.astype(f32))
    k_all = jnp.concatenate([past_k.astype(k.dtype), k], axis=1)
    v_all = jnp.concatenate([past_v.astype(v.dtype), v], axis=1)
    f_cum = jnp.cumsum(jnp.concatenate([past_logf.astype(f32), logf], axis=1), axis=1)
    fq = f_cum[:, past:]
    k_pos = jnp.arange(past + t)
    q_pos = past + jnp.arange(t)
    if q_block is None:
        o = fox_attend(q, k_all, v_all, fq, f_cum, q_pos, k_pos)
    else:
        nb = t // q_block
        qb = jnp.moveaxis(q.reshape(bsz, nb, q_block, N_ATTN_HEADS, ATTN_HEAD_DIM), 1, 0)
        fqb = jnp.moveaxis(fq.reshape(bsz, nb, q_block, N_ATTN_HEADS), 1, 0)
        o = lax.map(lambda blk: fox_attend(blk[0], k_all, v_all, blk[1], f_cum, blk[2], k_pos),
                    (qb, fqb, q_pos.reshape(nb, q_block)))
        o = jnp.moveaxis(o, 0, 1)
    y_attn = o.reshape(bsz, t, D_ATTN).astype(f32) * jax.nn.silu(z_attn.astype(f32))
    mixed = jnp.concatenate([y_ssm, y_attn], axis=-1).astype(x.dtype) @ w_out
    h = ALPHA * x.astype(f32) + mixed.astype(f32)
    mu = jnp.mean(h, axis=-1, keepdims=True)
    hc = h - mu
    var = jnp.mean(hc * hc, axis=-1, keepdims=True)
    out = (hc * lax.rsqrt(var + LN_EPS) * ln_g.astype(f32) + ln_b.astype(f32)).astype(x.dtype)
    return out, new_conv, new_ssm.astype(ssm_state.dtype), k, v, logf.astype(past_logf.dtype)


def setup_inputs(seed: int = 0) -> dict:
    key = jax.random.key(seed)
    ks = jax.random.split(key, 20)
    f32 = jnp.float32
    x_prompt = jax.random.normal(ks[0], (BATCH, SEQ, D_MODEL), f32)
    x_sample = jax.random.normal(ks[1], (DEC_BATCH, DEC_SEQ, D_MODEL), f32)
    cache_k = jax.random.normal(ks[2], (DEPTH, DEC_BATCH, PAST_LEN, N_ATTN_HEADS, ATTN_HEAD_DIM), f32)
    cache_v = jax.random.normal(ks[3], (DEPTH, DEC_BATCH, PAST_LEN, N_ATTN_HEADS, ATTN_HEAD_DIM), f32)
    cache_logf = jax.nn.log_sigmoid(jax.random.uniform(ks[4], (DEPTH, DEC_BATCH, PAST_LEN, N_ATTN_HEADS), f32, 1.0, 6.0))
    state_conv = jax.random.normal(ks[5], (DEPTH, DEC_BATCH, CONV_W - 1, D_XBC), f32)
    state_ssm = 0.5 * jax.random.normal(ks[6], (DEPTH, DEC_BATCH, N_SSM_HEADS, SSM_HEAD_DIM, D_STATE), f32)
    w_in = jax.random.normal(ks[7], (DEPTH, D_MODEL, D_IN_PROJ), f32) * D_MODEL ** -0.5
    conv_w = jax.random.normal(ks[8], (DEPTH, CONV_W, D_XBC), f32) * CONV_W ** -0.5
    conv_b = 0.01 * jax.random.normal(ks[9], (DEPTH, D_XBC), f32)
    dt0 = jnp.exp(jax.random.uniform(ks[10], (DEPTH, N_SSM_HEADS), f32, math.log(1e-3), math.log(1e-1)))
    dt_bias = dt0 + jnp.log(-jnp.expm1(-dt0))
    a_log = jnp.log(jax.random.uniform(ks[11], (DEPTH, N_SSM_HEADS), f32, 1.0, 16.0))
    d_skip = 1.0 + 0.1 * jax.random.normal(ks[12], (DEPTH, N_SSM_HEADS), f32)
    ssm_norm_w = 1.0 + 0.01 * jax.random.normal(ks[13], (DEPTH, D_SSM), f32)
    f_bias = jax.random.uniform(ks[14], (DEPTH, N_ATTN_HEADS), f32, 1.0, 6.0)
    w_out = jax.random.normal(ks[15], (DEPTH, D_MIX, D_MODEL), f32) * (D_MIX ** -0.5) * BETA
    ln_g = 1.0 + 0.01 * jax.random.normal(ks[16], (DEPTH, D_MODEL), f32)
    ln_b = 0.01 * jax.random.normal(ks[17], (DEPTH, D_MODEL), f32)
    return {'x_prompt': x_prompt, 'x_sample': x_sample, 'cache_k': cache_k, 'cache_v': cache_v,
            'cache_logf': cache_logf, 'state_conv': state_conv, 'state_ssm': state_ssm,
            'w_in': w_in, 'conv_w': conv_w, 'conv_b': conv_b, 'dt_bias': dt_bias, 'a_log': a_log,
            'd_skip': d_skip, 'ssm_norm_w': ssm_norm_w, 'f_bias': f_bias, 'w_out': w_out,
            'ln_g': ln_g, 'ln_b': ln_b}


def reference(x_prompt, x_sample, cache_k, cache_v, cache_logf, state_conv, state_ssm,
              w_in, conv_w, conv_b, dt_bias, a_log, d_skip, ssm_norm_w, f_bias, w_out, ln_g, ln_b):
    yp, ys = x_prompt, x_sample
    kp_l, vp_l, fp_l, cp_l, sp_l = [], [], [], [], []
    ks_l, vs_l, fs_l, cs_l, ss_l = [], [], [], [], []
    bp = x_prompt.shape[0]
    for layer in range(DEPTH):
        params = (w_in[layer], conv_w[layer], conv_b[layer], dt_bias[layer], a_log[layer], d_skip[layer],
                  ssm_norm_w[layer], f_bias[layer], w_out[layer], ln_g[layer], ln_b[layer])
        conv0 = jnp.zeros((bp, CONV_W - 1, D_XBC), yp.dtype)
        ssm0 = jnp.zeros((bp, N_SSM_HEADS, SSM_HEAD_DIM, D_STATE), state_ssm.dtype)
        k0 = jnp.zeros((bp, 0, N_ATTN_HEADS, ATTN_HEAD_DIM), yp.dtype)
        f0 = jnp.zeros((bp, 0, N_ATTN_HEADS), cache_logf.dtype)
        yp, cp, sp, kp, vp, fp = hybrid_layer(yp, conv0, ssm0, k0, k0, f0, *params,
                                              ssm_chunk=CHUNK, q_block=Q_BLOCK)
        ys, cs, ss, kn, vn, fn = hybrid_layer(ys, state_conv[layer], state_ssm[layer], cache_k[layer],
                                              cache_v[layer], cache_logf[layer], *params,
                                              ssm_chunk=x_sample.shape[1], q_block=None)
        kp_l.append(kp); vp_l.append(vp); fp_l.append(fp); cp_l.append(cp); sp_l.append(sp)
        ks_l.append(kn); vs_l.append(vn); fs_l.append(fn); cs_l.append(cs); ss_l.append(ss)
    return (yp, ys,
            jnp.stack(kp_l), jnp.stack(vp_l), jnp.stack(fp_l), jnp.stack(cp_l), jnp.stack(sp_l),
            jnp.stack(ks_l), jnp.stack(vs_l), jnp.stack(fs_l), jnp.stack(cs_l), jnp.stack(ss_l))
```

```python
import numpy as np
from contextlib import ExitStack
import concourse.bass as bass
import concourse.mybir as mybir
from concourse.bass_utils import run_bass_kernel_spmd

F32 = mybir.dt.float32
BF16 = mybir.dt.bfloat16
AF = mybir.ActivationFunctionType
ALU = mybir.AluOpType

D = 1024
KC = 8
DIN = 6688
ZS0, XBC0, DT0, Q0, K0, V0, ZA0, F0 = 0, 1024, 2560, 2576, 3600, 4624, 5648, 6672
ALPHA = 2.0 ** 0.25
LN_EPS = 1e-5
RMS_EPS = 1e-5
N_CORES = 8
import os as _os
KSTAGE = float(_os.environ.get("KSTAGE", "99"))


KVAR = int(_os.environ.get("KVAR", "3"))


class _Stop(Exception):
    pass


class Buf:
    def __init__(self, keys):
        self.keys = list(keys)


class _Rec:
    def __getattr__(self, name):
        return lambda *a, **k: (name, a, k)


_REC = _Rec()


class Em:
    CE = ['pe', 'act', 'dve', 'pool']

    def __init__(self, nc, es):
        self.nc = nc
        self.es = es
        self.ops = {e: [] for e in self.CE + ['sp']}
        self.sem = {e: es.enter_context(nc.semaphore("s_" + e)) for e in self.CE}
        self.seq = {e: 0 for e in self.CE}
        self.flag = {e: [] for e in self.CE}
        self.waited = {}
        self.lastw = {}
        self.readers = {}
        self.chan = {}

    def _keys(self, lst):
        out = []
        for b in lst:
            if isinstance(b, Buf):
                out.extend(b.keys)
            elif isinstance(b, (list, tuple)) and len(b) > 0 and isinstance(b[0], Buf):
                for x in b:
                    out.extend(x.keys)
            else:
                out.append(b)
        return out

    def _resolve(self, d):
        if d[0] == 'e':
            _, eng, seq = d
            fl = self.flag[eng]
            lo, hi = 0, len(fl)
            while lo < hi:
                mid = (lo + hi) // 2
                if fl[mid] >= seq:
                    hi = mid
                else:
                    lo = mid + 1
            assert lo < len(fl), f"dep on unflagged {eng} op {seq} with no later flagged op"
            return ('e', eng), self.sem[eng], lo + 1
        else:
            _, ch, cnt = d
            return ('d', ch), self.chan[ch][0], 16 * self.chan[ch][1]

    def _deps(self, eng, reads, writes):
        deps = []
        for k in reads:
            d = self.lastw.get(k)
            if d is not None:
                deps.append(d)
        for k in writes:
            d = self.lastw.get(k)
            if d is not None:
                deps.append(d)
            for r in self.readers.get(k, {}).values():
                deps.append(r)
        best = {}
        for d in deps:
            if d[0] == 'e' and d[1] == eng and eng == 'pe':
                continue
            src, sem, val = self._resolve(d)
            if self.waited.get((eng, src), 0) >= val:
                continue
            if src not in best or best[src][1] < val:
                best[src] = (sem, val)
        waits = []
        for src, (sem, val) in best.items():
            self.waited[(eng, src)] = val
            waits.append((sem, val))
        return waits

    def _record(self, me, srckey, reads, writes):
        for k in reads:
            self.readers.setdefault(k, {})[srckey] = me
        for k in writes:
            self.lastw[k] = me
            self.readers[k] = {}

    stopped = False

    def op(self, eng, fn, reads=(), writes=(), sig=True):
        if self.stopped:
            return
        reads = self._keys(reads)
        writes = self._keys(writes)
        waits = self._deps(eng, reads, writes)
        seq = self.seq[eng]
        self.seq[eng] += 1
        if sig:
            self.flag[eng].append(seq)
        name, a, k = fn(_REC)
        self.ops[eng].append((waits, (lambda e, name=name, a=a, k=k: getattr(e, name)(*a, **k)), sig))
        self._record(('e', eng, seq), eng, reads, writes)

    def dma(self, ch, out, in_, reads=(), writes=(), **kw):
        if self.stopped:
            return
        if ch not in self.chan:
            self.chan[ch] = [self.es.enter_context(self.nc.semaphore("d_" + ch)), 0]
        reads = self._keys(reads)
        writes = self._keys(writes)
        waits = self._deps('sp', reads, writes)
        c = self.chan[ch]
        c[1] += 1
        sem = c[0]

        def fn(e, out=out, in_=in_, sem=sem, kw=kw):
            return e.dma_start(out=out, in_=in_, **kw).then_inc(sem, 16)
        self.ops['sp'].append((waits, fn, False))
        self._record(('d', ch, c[1]), ('d', ch), reads, writes)

    def replay(self, block):
        em = self

        def run(eng_name):
            def body(e):
                sem = em.sem.get(eng_name)
                for waits, fn, sig in em.ops[eng_name]:
                    for s, v in waits:
                        e.wait_ge(s, v)
                    ins = fn(e)
                    if sig:
                        ins.then_inc(sem, 1)
                if eng_name == 'sp':
                    for ch, (s, cnt) in em.chan.items():
                        if cnt > 0:
                            e.wait_ge(s, 16 * cnt)
                    for ce in em.CE:
                        n = len(em.flag[ce])
                        if n > 0:
                            e.wait_ge(em.sem[ce], n)
            return body
        block.tensor(run('pe'))
        block.scalar(run('act'))
        block.vector(run('dve'))
        block.gpsimd(run('pool'))
        block.sync(run('sp'))


def build_program(NSEQ, T, PAST, TS):
    TB = 512
    assert T % TB == 0
    KMAX = ((max(T, PAST + TS) + 511) // 512) * 512
    NKB = max(T // 128, PAST // 128 + 1)
    nc = bass.Bass("TRN2", target_bir_lowering=False)

    def din(name, shape):
        return nc.dram_tensor(name, list(shape), F32, kind="ExternalInput").ap()

    def dout(name, shape):
        return nc.dram_tensor(name, list(shape), F32, kind="ExternalOutput").ap()

    xT_p = din("xT_p", [NSEQ, D, T]); x_p = din("x_p", [NSEQ, T, D])
    xT_s = din("xT_s", [D, TS]); x_s = din("x_s", [TS, D])
    ckT = din("ckT", [8, 128, PAST]); cv = din("cv", [PAST, D]); clf = din("clf", [PAST, 16])
    sconv = din("sconv", [128, 12, 3]); sssm = din("sssm", [128, D])
    w_in = din("w_in", [KC, 128, DIN]); w_out = din("w_out", [16, 128, D]); wdtf_f = din("wdtf_f", [KC, 128, 32])
    convw_d = din("convw", [128, 12, 4]); convb_d = din("convb", [128, 12])
    dtb_d = din("dtb", [16]); alog_d = din("alog", [16]); fb_d = din("fb", [16])
    dcol_d = din("dcol", [128, 8]); normw_d = din("normw", [128, 8])
    lng_d = din("lng", [D]); lnb_d = din("lnb", [D])

    y_p = dout("y_p", [NSEQ, T, D]); y_s = dout("y_s", [TS, D])
    k_p = dout("k_p", [NSEQ, 8, 128, T]); v_p = dout("v_p", [NSEQ, T, D]); lf_p = dout("lf_p", [NSEQ, T, 16])
    conv_p = dout("conv_p", [NSEQ, 128, 12, 3]); ssm_p = dout("ssm_p", [NSEQ, 128, D])
    k_s = dout("k_s", [8, 128, TS]); v_s = dout("v_s", [TS, D]); lf_s = dout("lf_s", [TS, 16])
    conv_s = dout("conv_s", [128, 12, 3]); ssm_s = dout("ssm_s", [128, D])

    GROUP_COLS = [512 * g for g in range(5)] + [K0, K0 + 512, V0, V0 + 512, Q0, Q0 + 512, ZA0, ZA0 + 512]
    wbf = nc.dram_tensor("wbf", [13, 128, KC * 512], BF16, kind="Internal").ap()
    kscr = nc.dram_tensor("kscr", [16, 66, KMAX], BF16, kind="Internal").ap()
    vscr = nc.dram_tensor("vscr", [8, 128, NKB, 130], BF16, kind="Internal").ap()

    with ExitStack() as es:
        em = Em(nc, es)

        def sb(name, shape, dt):
            return es.enter_context(nc.sbuf_tensor(name, list(shape), dt))

        identb = sb("identb", [128, 128], BF16); B_identb = Buf(["identb"])
        tri = sb("tri", [128, 128], F32)
        ustr = sb("ustr", [128, 128], F32)
        ones_f = sb("ones_f", [128, 128], F32)
        ones_b = sb("ones_b", [128, 128], BF16)
        cmaskb = sb("cmaskb", [128, 128], BF16)
        negmask = sb("negmask", [128, 128], BF16)
        B_const = Buf(["consts"])
        wdtf = sb("wdtf", [128, KC, 32], BF16); B_wdtf = Buf(["wdtf"])
        convw = sb("convw_t", [128, 12, 4], F32); convb = sb("convb_t", [128, 12], F32)
        dcol = sb("dcol_t", [128, 8], F32); normw = sb("normw_t", [128, 8], F32)
        dtb = sb("dtb_t", [128, 16], F32); Abc = sb("Abc", [128, 16], F32); fbb = sb("fbb", [128, 16], F32)
        lng = sb("lng_t", [128, D], F32); lnb = sb("lnb_t", [128, D], F32)
        B_par = Buf(["params"])
        wo = sb("wo", [128, 16, D], BF16); B_wo = [Buf([("wo", i)]) for i in range(4)]
        hT = sb("hT", [128, D], F32); B_hT = Buf(["hT"])
        hTb = sb("hTb", [128, D], BF16); B_hTb = Buf(["hTb"])
        ccar = sb("ccar", [128, 12, 3], F32); B_ccar = [Buf([("ccar", c)]) for c in range(12)]
        Fcar = sb("Fcar", [128, 16], F32); B_Fcar = Buf(["Fcar"])
        Fall = sb("Fall", [128, NKB, 16], F32); B_Fall = [Buf([("Fall", i)]) for i in range(NKB)]
        Fend = sb("Fend", [128, 4, 16], F32); B_Fend = [Buf([("Fend", i)]) for i in range(4)]

        xT = sb("xT", [128, KC, TB], BF16); B_xT = Buf(["xT"])
        wsl = sb("wsl", [128, 2, KC, 512], BF16); B_wsl = [Buf([("wsl", i)]) for i in range(2)]
        qA = sb("qA", [128, 16, TB], BF16); B_qA = [Buf([("qA", h)]) for h in range(16)]
        onesrow = sb("onesrow", [128, 512], BF16)
        dhl = sb("dhl", [128, 4, 16, 2], BF16); B_dhl = [Buf([("dhl", j)]) for j in range(4)]
        dlt = sb("dlt", [128, 4, 16], F32); B_dlt = [Buf([("dlt", j)]) for j in range(4)]
        Fblk0 = sb("Fblk0", [128, 16], F32); B_Fblk0 = Buf(["Fblk0"])
        zsT = sb("zsT", [128, 8, TB], BF16); B_zsT = [Buf([("zsT", c)]) for c in range(8)]
        zaT = sb("zaT", [128, 8, TB], BF16); B_zaT = [Buf([("zaT", c)]) for c in range(8)]
        kTc = zaT; B_kTc = B_zaT
        xcT = sb("xcT", [128, 8, TB], BF16); B_xcT = [Buf([("xcT", c)]) for c in range(8)]
        BCT = sb("BCT", [128, 4, TB], BF16); B_BCT = [Buf([("BCT", c)]) for c in range(4)]
        mixT = sb("mixT", [128, 16, TB], BF16); B_mix = [Buf([("mix", c)]) for c in range(16)]
        dtk = sb("dtk", [128, 4, 16], F32); B_dtk = [Buf([("dtk", j)]) for j in range(4)]
        dAk = sb("dAk", [128, 4, 16], F32); B_dAk = [Buf([("dAk", j)]) for j in range(4)]
        lft = sb("lft", [128, 4, 16], F32); B_lft = [Buf([("lft", j)]) for j in range(4)]
        smallt = sb("smallt", [128, 4, 64], F32); B_small = [Buf([("small", j)]) for j in range(4)]

        UBYTES = 52 * 1024
        U = sb("U", [128, UBYTES // 4], F32)

        def carve(off, nbytes, dt, pattern=None, **kw):
            assert off % 4 == 0 and nbytes % 4 == 0 and off + nbytes <= UBYTES, (off, nbytes)
            ap = U[:, off // 4:(off + nbytes) // 4]
            if dt != F32:
                ap = ap.bitcast(dt)
            if pattern:
                ap = ap.rearrange(pattern, **kw)
            keys = [("U", i) for i in range(off // 1024, (off + nbytes - 1) // 1024 + 1)]
            return ap, Buf(keys)

        KB = 1024
        xTf, B_xTf = carve(0, 16 * KB, F32, "p (k t) -> p k t", k=KC)
        cst = [carve(16 * KB + i * 3 * KB, 2060, F32) for i in range(2)]
        cacc = [carve(22 * KB + i * 2 * KB, 2 * KB, F32) for i in range(2)]
        kf = [(sb("kf%d" % i, [128, 512], F32), Buf([("kf", i)])) for i in range(2)]
        vf = [(sb("vf%d" % i, [128, 512], F32), Buf([("vf", i)])) for i in range(2)]
        vb = [(sb("vb%d" % i, [128, 8, 65], BF16), Buf([("vb", i)])) for i in range(2)]
        stg = [carve(i * 16 * KB, 16 * KB, F32, "p (k c) -> p k c", k=KC) for i in range(2)]
        stgb = [carve(32 * KB + i * 8 * KB, 8 * KB, BF16, "p (k c) -> p k c", k=KC) for i in range(2)]
        Rg2 = [carve(0, 4 * KB, F32, "p (e q) -> p e q", e=8), carve(37 * KB, 4 * KB, F32, "p (e q) -> p e q", e=8)]
        expd2 = [carve(4 * KB, 4 * KB, F32, "p (e q) -> p e q", e=8), carve(41 * KB, 4 * KB, F32, "p (e q) -> p e q", e=8)]
        Eg2 = [carve(8 * KB, 4 * KB, F32, "p (e q) -> p e q", e=8), carve(45 * KB, 4 * KB, F32, "p (e q) -> p e q", e=8)]
        Mb, B_Mb = carve(12 * KB, 4 * KB, BF16, "p (h q) -> p h q", h=16)
        Csb, B_Csb = carve(16 * KB, 4 * KB, BF16, "p (h q) -> p h q", h=16)
        xdt, B_xdt = carve(20 * KB, 2 * KB, BF16)
        xdte, B_xdte = carve(22 * KB, 2 * KB, BF16)
        Btok, B_Btok = carve(24 * KB, 512, BF16)
        CBm, B_CBm = carve(25 * KB, 1 * KB, F32, "p (g q) -> p g q", g=2)
        t1, B_t1 = carve(26 * KB, 4 * KB, F32, "p (c q) -> p c q", c=8)
        gT, B_gT = carve(30 * KB, 4 * KB, F32, "p (c q) -> p c q", c=8)
        gsq, B_gsq = carve(34 * KB, 2 * KB, BF16, "p (c q) -> p c q", c=8)
        rstd, B_rstd = carve(36 * KB, 1 * KB, F32, "p (g q) -> p g q", g=2)
        ktS = [carve(i * 8 * KB, 8 * KB, BF16) for i in range(2)]
        vS = [carve(16 * KB + i * 9 * KB, NKB * 130 * 2, BF16, "p (b d) -> p b d", d=130) for i in range(2)]
        assert NKB * 130 * 2 <= 9 * KB
        PTs = [carve(34 * KB + i * KB, KB, BF16) for i in range(4)]
        Bj = [carve(38 * KB + i * 2 * KB, NKB * 16 * 4, F32, "p (b h) -> p b h", h=16) for i in range(4)]
        assert NKB * 64 <= 2 * KB
        rl2 = [carve(46 * KB, 2 * KB, F32), carve(40 * KB, 2 * KB, F32)]
        rlb2 = [carve(48 * KB, 2 * KB, F32), carve(42 * KB, 2 * KB, F32)]
        onb2 = [carve(50 * KB, 2 * KB, F32), carve(44 * KB, 2 * KB, F32)]
        xres = [carve(34 * KB + i * 4 * KB, 4 * KB, F32) for i in range(2)]
        hb = [carve(42 * KB + i * 4 * KB, 4 * KB, F32) for i in range(2)]
        lnst = [carve(50 * KB + i * KB, 256, F32) for i in range(2)]

        PS = [es.enter_context(nc.psum_tensor("ps%d" % i, [128, 1024], F32)) for i in range(4)]
        B_bank = [Buf([("bank", i)]) for i in range(8)]

        def bank(i):
            return PS[i // 2][:, (i % 2) * 512:(i % 2) * 512 + 512]

        def bank_bf(i):
            return bank(i).bitcast(BF16)

        em.op('pool', lambda e: e.memset(tri[:], 1.0), writes=[B_const])
        em.op('pool', lambda e: e.affine_select(out=tri[:], in_=tri[:], pattern=[[1, 128]], compare_op=ALU.is_ge,
                                                 fill=0.0, base=0, channel_multiplier=-1), reads=[B_const], writes=[B_const])
        em.op('pool', lambda e: e.memset(ustr[:], 1.0), writes=[B_const])
        em.op('pool', lambda e: e.affine_select(out=ustr[:], in_=ustr[:], pattern=[[-1, 128]], compare_op=ALU.is_gt,
                                                 fill=0.0, base=0, channel_multiplier=1), reads=[B_const], writes=[B_const])
        em.op('pool', lambda e: e.memset(ones_f[:], 1.0), writes=[B_const])
        em.op('pool', lambda e: e.memset(ones_b[:], 1.0), writes=[B_const])
        em.op('pool', lambda e: e.tensor_copy(out=cmaskb[:], in_=tri[:]), reads=[B_const], writes=[B_const])
        em.op('pool', lambda e: e.tensor_scalar(out=negmask[:], in0=ustr[:], scalar1=-30000.0, scalar2=None, op0=ALU.mult),
              reads=[B_const], writes=[B_const])
        em.op('pool', lambda e: e.memset(identb[:], 1.0), writes=[B_identb])
        em.op('pool', lambda e: e.affine_select(out=identb[:], in_=identb[:], pattern=[[1, 128]], compare_op=ALU.is_equal,
                                                 fill=0.0, base=0, channel_multiplier=-1), reads=[B_identb], writes=[B_identb])
        for i in range(2):
            em.op('pool', lambda e, i=i: e.memset(vb[i][0][:], 1.0), writes=[vb[i][1]])
        em.op('pool', lambda e: e.memset(Fall[:], 0.0), writes=B_Fall)
        em.op('pool', lambda e: e.memset(onesrow[:], 1.0), writes=["onesrow"])
        for h in range(16):
            em.dma("ks%d" % h, kscr[h, 64:66, :].rearrange("p (r c) -> p r c", c=512),
                   onesrow[0:2, :].unsqueeze(1).broadcast_to([2, KMAX // 512, 512]), reads=["onesrow"], writes=[("kscr", h)])
        em.dma("par", convw[:], convw_d[:, :, :], writes=[B_par])
        em.dma("par", convb[:], convb_d[:, :], writes=[B_par])
        em.dma("par", dcol[:], dcol_d[:, :], writes=[B_par])
        em.dma("par", normw[:], normw_d[:, :], writes=[B_par])
        em.dma("par", dtb[:], dtb_d.partition_broadcast(128), writes=[B_par])
        em.dma("par", Abc[:], alog_d.partition_broadcast(128), writes=[B_par])
        em.dma("par", fbb[:], fb_d.partition_broadcast(128), writes=[B_par])
        em.dma("par", lng[:], lng_d.partition_broadcast(128), writes=[B_par])
        em.dma("par", lnb[:], lnb_d.partition_broadcast(128), writes=[B_par])
        em.op('act', lambda e: e.activation(out=Abc[:], in_=Abc[:], func=AF.Exp), reads=[B_par], writes=[B_par])
        em.op('dve', lambda e: e.tensor_scalar(out=Abc[:], in0=Abc[:], scalar1=-1.0, scalar2=None, op0=ALU.mult),
              reads=[B_par], writes=[B_par])

        def _ck(n):
            if KSTAGE <= n:
                em.stopped = True
        cast_engs = ['dve', 'act', 'dve', 'act', 'pool']
        _ck(1)

        def cast(eng, out, in_, reads, writes):
            if eng == 'act':
                em.op('act', lambda e: e.activation(out=out, in_=in_, func=AF.Copy), reads=reads, writes=writes)
            else:
                em.op(eng, lambda e: e.tensor_copy(out=out, in_=in_), reads=reads, writes=writes)

        for p, c0 in enumerate(GROUP_COLS):
            w = 512
            s = p % 2
            sa, sB = stg[s]
            ba, bB = stgb[s]
            em.dma("wl%d" % s, sa[:, :, 0:w], w_in[:, :, c0:c0 + w].rearrange("k p c -> p k c"), writes=[sB])
            for hlf in range(2):
                cast(cast_engs[(2 * p + hlf) % 5], ba[:, 4 * hlf:4 * hlf + 4, 0:w], sa[:, 4 * hlf:4 * hlf + 4, 0:w],
                     [sB], [bB])
            em.dma("ws%d" % s, wbf[p, :, :], ba.rearrange("p k c -> p (k c)"), reads=[bB], writes=["wbf"])
        for p in range(4):
            s = p % 2
            sa, sB = stg[s]
            sv = sa.rearrange("p k c -> p (k c)").rearrange("p (f d) -> p f d", f=4)
            em.dma("wl%d" % s, sv, w_out[4 * p:4 * p + 4, :, :].rearrange("f p d -> p f d"), writes=[sB])
            for hlf in range(2):
                cast(cast_engs[(2 * p + hlf) % 5], wo[:, 4 * p + 2 * hlf:4 * p + 2 * hlf + 2, :],
                     sv[:, 2 * hlf:2 * hlf + 2, :], [sB], [B_wo[p]])
        sa, sB = stg[0]
        em.dma("wl0", sa[:, :, 0:32], wdtf_f[:, :, :].rearrange("k p c -> p k c"), writes=[sB])
        cast('dve', wdtf[:], sa[:, :, 0:32], [sB], [B_wdtf])

        _ck(2)
        fm_ctr = [0]

        def f_update(lf_ap, B_lf, SB, kbi, fe_idx):
            psF = bank(4)
            em.op('pe', lambda e: e.matmul(psF[0:SB, 256:272], lhsT=tri[0:SB, 0:SB], rhs=lf_ap, start=True, stop=True),
                  reads=[B_lf, B_const], writes=[B_bank[4]], sig=False)
            em.op('pe', lambda e: e.matmul(psF[0:128, 272:288], lhsT=ones_f[0:SB, 0:128], rhs=lf_ap, start=True, stop=True),
                  reads=[B_lf, B_const], writes=[B_bank[4]])
            em.op('dve', lambda e: e.tensor_tensor(out=Fall[0:SB, kbi, :], in0=psF[0:SB, 256:272], in1=Fcar[0:SB, :], op=ALU.add),
                  reads=[B_bank[4], B_Fcar], writes=[B_Fall[kbi]])
            em.op('dve', lambda e: e.tensor_tensor(out=Fend[:, fe_idx, :], in0=psF[:, 272:288], in1=Fcar[:, :], op=ALU.add),
                  reads=[B_bank[4], B_Fcar], writes=[B_Fend[fe_idx]])
            em.op('pool', lambda e: e.tensor_copy(out=Fcar[:, :], in_=Fend[:, fe_idx, :]), reads=[B_Fend[fe_idx]], writes=[B_Fcar])

        w_pref = [False]

        def prefetch_x(xT_src, t0, TBK):
            em.dma("xT", xTf[:, :, 0:TBK], xT_src[:, t0:t0 + TBK].rearrange("(k p) t -> p k t", p=128), writes=[B_xTf])
            em.op('dve', lambda e: e.tensor_copy(out=xT[:, 0:4, 0:TBK], in_=xTf[:, 0:4, 0:TBK]), reads=[B_xTf], writes=[B_xT])
            em.op('act', lambda e: e.activation(out=xT[:, 4:8, 0:TBK], in_=xTf[:, 4:8, 0:TBK], func=AF.Copy), reads=[B_xTf], writes=[B_xT])

        def do_block(xT_src, x_src, t0, TBK, SB, past, outs, last_block, first=False, nxt=None):
            y_o, k_o, v_o, lf_o, conv_o = outs
            nsub = TBK // SB
            key0 = past + t0
            kb0 = key0 // 128
            nkeys = key0 + TBK
            nkb = kb0 + nsub

            def koff(kb):
                return kb * 128 if kb < kb0 else kb0 * 128 + (kb - kb0) * SB

            em.op('pool', lambda e: e.tensor_copy(out=Fblk0[:, :], in_=Fcar[:, :]), reads=[B_Fcar], writes=[B_Fblk0])
            if first:
                prefetch_x(xT_src, t0, TBK)

            psd = bank(4)
            pieces = []

            def piece_dtf():
                for j in range(nsub):
                    for kc in range(KC):
                        em.op('pe', lambda e, j=j, kc=kc: e.matmul(psd[0:SB, 32 * j:32 * j + 32], lhsT=xT[:, kc, j * SB:(j + 1) * SB],
                                                                   rhs=wdtf[:, kc, :], start=(kc == 0), stop=(kc == KC - 1)),
                              reads=[B_xT, B_wdtf], writes=[B_bank[4]], sig=(kc == KC - 1))
            pieces.append(piece_dtf)

            def piece_f(j):
                sm = smallt[0:SB, j, :]
                Bs = B_small[j]
                em.op('dve', lambda e, j=j, sm=sm: e.tensor_tensor(out=sm[:, 0:16], in0=psd[0:SB, 32 * j + 16:32 * j + 32],
                                                                  in1=fbb[0:SB, :], op=ALU.add),
                      reads=[B_bank[4], B_par], writes=[Bs])
                em.op('dve', lambda e, j=j, sm=sm: e.tensor_tensor(out=sm[:, 16:32], in0=psd[0:SB, 32 * j:32 * j + 16],
                                                                  in1=dtb[0:SB, :], op=ALU.add),
                      reads=[B_bank[4], B_par], writes=[Bs])
                em.op('act', lambda e, sm=sm: e.activation(out=sm[:, 0:16], in_=sm[:, 0:16], func=AF.Exp, scale=-1.0),
                      reads=[Bs], writes=[Bs])
                em.op('act', lambda e, sm=sm: e.activation(out=sm[:, 16:32], in_=sm[:, 16:32], func=AF.Exp),
                      reads=[Bs], writes=[Bs])
                em.op('act', lambda e, sm=sm: e.activation(out=sm[:, 0:16], in_=sm[:, 0:16], func=AF.Ln, bias=1.0),
                      reads=[Bs], writes=[Bs])
                em.op('act', lambda e, j=j, sm=sm: e.activation(out=dtk[0:SB, j, :], in_=sm[:, 16:32], func=AF.Ln, bias=1.0),
                      reads=[Bs], writes=[B_dtk[j]])
                em.op('dve', lambda e, j=j, sm=sm: e.tensor_scalar(out=lft[0:SB, j, :], in0=sm[:, 0:16], scalar1=-1.0, scalar2=None,
                                                                  op0=ALU.mult), reads=[Bs], writes=[B_lft[j]])
                em.op('dve', lambda e, j=j: e.tensor_tensor(out=dAk[0:SB, j, :], in0=dtk[0:SB, j, :], in1=Abc[0:SB, :], op=ALU.mult),
                      reads=[B_dtk[j], B_par], writes=[B_dAk[j]])
                em.dma("lfo%d" % j, lf_o[t0 + j * SB:t0 + (j + 1) * SB, :], lft[0:SB, j, :], reads=[B_lft[j]])

            def piece_fB(j):
                f_update(lft[0:SB, j, :], B_lft[j], SB, kb0 + j, j)

            def _mk_f(j):
                def run():
                    if j >= 1:
                        piece_fB(j - 1)
                    if j < nsub:
                        piece_f(j)
                return run
            for j in range(nsub + 1):
                pieces.append(_mk_f(j))

            def piece_delta_a():
                for j in range(nsub):
                    em.op('dve', lambda e, j=j: e.tensor_tensor(out=dlt[0:SB, j, :], in0=Fall[0:SB, kb0 + j, :], in1=Fblk0[0:SB, :],
                                                                op=ALU.subtract), reads=[B_Fall[kb0 + j], B_Fblk0], writes=[B_dlt[j]])
                    em.op('dve', lambda e, j=j: e.tensor_copy(out=dhl[0:SB, j, :, 0], in_=dlt[0:SB, j, :]), reads=[B_dlt[j]], writes=[B_dhl[j]])
                    em.op('dve', lambda e, j=j: e.tensor_tensor(out=dhl[0:SB, j, :, 1], in0=dlt[0:SB, j, :], in1=dhl[0:SB, j, :, 0],
                                                                op=ALU.subtract), reads=[B_dlt[j], B_dhl[j]], writes=[B_dhl[j]])
            pieces.append(piece_delta_a)
            def delta_mm(r0, bq):
                psQ = bank(bq)
                hs = list(range(r0, min(16, r0 + 3)))
                for i, h in enumerate(hs):
                    for j in range(nsub):
                        em.op('pe', lambda e, i=i, h=h, j=j: e.matmul(
                            psQ[32 * i:32 * i + 2, j * SB:(j + 1) * SB], lhsT=dhl[0:SB, j, h, :], rhs=identb[0:SB, 0:SB],
                            start=True, stop=True), reads=[B_dhl[j], B_identb], writes=[B_bank[bq]],
                            sig=(i == len(hs) - 1 and j == nsub - 1))

            def delta_cp(r0, bq):
                psQ = bank(bq)
                hs = list(range(r0, min(16, r0 + 3)))
                for i, h in enumerate(hs):
                    em.op('act', lambda e, i=i, h=h: e.activation(out=qA[64:66, h, 0:TBK], in_=psQ[32 * i:32 * i + 2, 0:TBK], func=AF.Copy),
                          reads=[B_bank[bq]], writes=[B_qA[h]])

            rounds = list(range(0, 16, 3))

            def _mk_d(k):
                def run():
                    for t, r0 in enumerate(rounds[2 * k:2 * k + 2]):
                        delta_mm(r0, 7 - t)
                    for t, r0 in enumerate(rounds[2 * k:2 * k + 2]):
                        delta_cp(r0, 7 - t)
                return run
            for k in range((len(rounds) + 1) // 2):
                pieces.append(_mk_d(k))
            _ck(3)
            groups = []
            for g in range(5):
                groups.append((512 * g, 'fm', g))
            for g in range(2):
                groups.append((K0 + 512 * g, 'fm', 7 + g))
            for g in range(2):
                groups.append((V0 + 512 * g, 'v', g))
            for g in range(2):
                groups.append((Q0 + 512 * g, 'fm', 5 + g))
            for g in range(2):
                groups.append((ZA0 + 512 * g, 'fm', 9 + g))
            SSD_GI = 5

            def load_group(gi):
                c0 = groups[gi][0]
                s = gi % 2
                assert GROUP_COLS[gi] == c0
                em.dma("wg%d" % s, wsl[:, s, :, :].rearrange("p k c -> p (k c)"), wbf[gi, :, :], reads=["wbf"],
                       writes=[B_wsl[s]])

            def ssd_gen():
              for j in range(nsub):
                  cols = slice(j * SB, (j + 1) * SB)
                  psX = bank_bf(5)
                  psB = bank_bf(6)
                  psCB = bank(6)
                  for c in range(8):
                      em.op('pe', lambda e, c=c: e.transpose(psX[0:SB, 128 * c:128 * c + 128], xcT[:, c, cols], identb[:]),
                            reads=[B_xcT[c], B_identb], writes=[B_bank[5]], sig=(c == 7))
                  for g in range(2):
                      em.op('pe', lambda e, g=g: e.transpose(psB[0:SB, 128 * g:128 * g + 128], BCT[:, g, cols], identb[:]),
                            reads=[B_BCT[g], B_identb], writes=[B_bank[6]], sig=(g == 1))
                  for g in range(2):
                      em.op('pe', lambda e, g=g: e.matmul(psCB[0:SB, 256 + 128 * g:256 + 128 * g + SB], lhsT=BCT[:, g, cols],
                                                          rhs=BCT[:, 2 + g, cols], start=True, stop=True),
                            reads=[B_BCT[g], B_BCT[2 + g]], writes=[B_bank[6]], sig=(g == 1))
                  em.op('dve', lambda e, j=j: e.tensor_tensor(
                      out=xdt[0:SB, :].rearrange("p (h d) -> p h d", h=16),
                      in0=psX[0:SB, :].rearrange("p (h d) -> p h d", h=16),
                      in1=dtk[0:SB, j, :].unsqueeze(2).broadcast_to([SB, 16, 64]), op=ALU.mult),
                      reads=[B_bank[5], B_dtk[j]], writes=[B_xdt])
                  em.op('dve', lambda e: e.tensor_copy(out=Btok[0:SB, :], in_=psB[0:SB, 0:256]),
                        reads=[B_bank[6]], writes=[B_Btok])
                  em.op('dve', lambda e: e.tensor_tensor(
                      out=CBm[0:SB, :, 0:SB], in0=psCB[0:SB, 256:512].rearrange("p (g q) -> p g q", g=2)[:, :, 0:SB],
                      in1=tri[0:SB, 0:SB].unsqueeze(1).broadcast_to([SB, 2, SB]), op=ALU.mult),
                      reads=[B_bank[6], B_const], writes=[B_CBm])
                  W8 = 8 * SB
                  nmm = (W8 + 511) // 512
                  for g in range(2):
                      em.op('pool', lambda e, g=g, j=j: e.tensor_tensor(
                          out=Rg2[g][0][0:SB, :, 0:SB], in0=dAk[0:SB, j, 8 * g:8 * g + 8].unsqueeze(2).broadcast_to([SB, 8, SB]),
                          in1=tri[0:SB, 0:SB].unsqueeze(1).broadcast_to([SB, 8, SB]), op=ALU.mult),
                          reads=[B_dAk[j], B_const], writes=[Rg2[g][1]])
                  yield
                  for g in range(2):
                      Rga, B_Rga = Rg2[g]
                      exa, B_exa = expd2[g]
                      ega, B_ega = Eg2[g]
                      for m in range(nmm):
                          e0 = m * (512 // SB)
                          ne = min(8, e0 + 512 // SB) - e0
                          wcols = ne * SB
                          bd, be = m, 2 + m
                          em.op('pe', lambda e: e.matmul(
                              bank(bd)[0:SB, 0:wcols], lhsT=ustr[0:SB, 0:SB], rhs=Rga[0:SB, e0:e0 + ne, 0:SB], start=True, stop=True),
                              reads=[B_Rga, B_const], writes=[B_bank[bd]])
                          em.op('pe', lambda e: e.matmul(
                              bank(be)[0:128, 0:wcols], lhsT=ones_f[0:SB, 0:128], rhs=Rga[0:SB, e0:e0 + ne, 0:SB], start=True, stop=True),
                              reads=[B_Rga, B_const], writes=[B_bank[be]])
                      for m in range(nmm):
                          e0 = m * (512 // SB)
                          ne = min(8, e0 + 512 // SB) - e0
                          wcols = ne * SB
                          bd, be = m, 2 + m
                          em.op('act', lambda e: e.activation(
                              out=exa[0:SB, e0:e0 + ne, 0:SB], in_=bank(bd)[0:SB, 0:wcols].rearrange("p (e q) -> p e q", e=ne), func=AF.Exp),
                              reads=[B_bank[bd]], writes=[B_exa])
                          em.op('act', lambda e: e.activation(
                              out=ega[:, e0:e0 + ne, 0:SB], in_=bank(be)[:, 0:wcols].rearrange("p (e q) -> p e q", e=ne), func=AF.Exp),
                              reads=[B_bank[be]], writes=[B_ega])
                  for g in range(2):
                      exa, B_exa = expd2[g]
                      ega, B_ega = Eg2[g]
                      em.op('dve', lambda e, g=g: e.tensor_tensor(
                          out=Mb[0:SB, 8 * g:8 * g + 8, 0:SB], in0=exa[0:SB, :, 0:SB],
                          in1=CBm[0:SB, g, 0:SB].unsqueeze(1).broadcast_to([SB, 8, SB]), op=ALU.mult),
                          reads=[B_exa, B_CBm], writes=[B_Mb])
                      em.op('dve', lambda e, g=g: e.tensor_tensor(
                          out=Csb[:, 8 * g:8 * g + 8, 0:SB], in0=ega[:, :, 0:SB],
                          in1=BCT[:, 2 + g, cols].unsqueeze(1).broadcast_to([128, 8, SB]), op=ALU.mult),
                          reads=[B_ega, B_BCT[2 + g]], writes=[B_Csb])
                      em.op('pool', lambda e, g=g: e.tensor_tensor(
                          out=xdte[0:SB, 512 * g:512 * g + 512].rearrange("p (h d) -> p h d", h=8),
                          in0=xdt[0:SB, 512 * g:512 * g + 512].rearrange("p (h d) -> p h d", h=8),
                          in1=exa[0:SB, :, SB - 1:SB].broadcast_to([SB, 8, 64]), op=ALU.mult),
                          reads=[B_xdt, B_exa], writes=[B_xdte])
                      em.op('pool', lambda e, g=g: e.tensor_tensor(
                          out=hT[:, 512 * g:512 * g + 512].rearrange("p (h d) -> p h d", h=8),
                          in0=hT[:, 512 * g:512 * g + 512].rearrange("p (h d) -> p h d", h=8),
                          in1=ega[:, :, SB - 1:SB].broadcast_to([128, 8, 64]), op=ALU.mult),
                          reads=[B_hT, B_ega], writes=[B_hT])
                  yield
                  psY = PS[0]
                  for h in range(16):
                      c, hp = h // 2, h % 2
                      bi = (c * SB) // 512
                      em.op('pe', lambda e, h=h, c=c, hp=hp: e.matmul(
                          psY[64 * hp:64 * hp + 64, c * SB:(c + 1) * SB], lhsT=xdt[0:SB, 64 * h:64 * h + 64], rhs=Mb[0:SB, h, 0:SB],
                          start=True, stop=False), reads=[B_xdt, B_Mb], writes=[B_bank[bi]], sig=False)
                      em.op('pe', lambda e, h=h, c=c, hp=hp: e.matmul(
                          psY[64 * hp:64 * hp + 64, c * SB:(c + 1) * SB], lhsT=hTb[:, 64 * h:64 * h + 64], rhs=Csb[:, h, 0:SB],
                          start=False, stop=True), reads=[B_hTb, B_Csb], writes=[B_bank[bi]], sig=True)
                  psS = PS[1]
                  for g in range(2):
                      em.op('pe', lambda e, g=g: e.matmul(psS[:, 512 * g:512 * g + 512], lhsT=Btok[0:SB, 128 * g:128 * g + 128],
                                                          rhs=xdte[0:SB, 512 * g:512 * g + 512], start=True, stop=True),
                            reads=[B_Btok, B_xdte], writes=[B_bank[2 + g]])
                  for g in range(2):
                      em.op('dve', lambda e, g=g: e.tensor_tensor(out=hT[:, 512 * g:512 * g + 512], in0=psS[:, 512 * g:512 * g + 512],
                                                                  in1=hT[:, 512 * g:512 * g + 512], op=ALU.add),
                            reads=[B_bank[2 + g], B_hT], writes=[B_hT])
                  em.op('act', lambda e: e.activation(out=hTb[:, :], in_=hT[:, :], func=AF.Copy), reads=[B_hT], writes=[B_hTb])
                  nb = (8 * SB + 511) // 512
                  em.op('pool', lambda e: e.tensor_tensor(out=t1[:, :, 0:SB], in0=xcT[:, :, cols],
                                                          in1=dcol[:, :].unsqueeze(2).broadcast_to([128, 8, SB]), op=ALU.mult),
                        reads=B_xcT + [B_par], writes=[B_t1])
                  em.op('dve', lambda e: e.tensor_tensor(out=t1[:, :, 0:SB], in0=psY[:, 0:8 * SB].rearrange("p (c q) -> p c q", c=8),
                                                         in1=t1[:, :, 0:SB], op=ALU.add),
                        reads=[B_t1] + [B_bank[i] for i in range(nb)], writes=[B_t1])
                  em.op('dve', lambda e: e.tensor_tensor(out=gT[:, :, 0:SB], in0=t1[:, :, 0:SB], in1=zsT[:, :, cols], op=ALU.mult),
                        reads=[B_t1] + B_zsT, writes=[B_gT])
                  em.op('act', lambda e: e.activation(out=gsq[:, :, 0:SB], in_=gT[:, :, 0:SB], func=AF.Square),
                        reads=[B_gT], writes=[B_gsq])
                  yield
                  psR = bank(5)
                  for g2 in range(2):
                      for cc in range(4):
                          em.op('pe', lambda e, g2=g2, cc=cc: e.matmul(psR[:, g2 * SB:(g2 + 1) * SB], lhsT=ones_b[:, :],
                                                                       rhs=gsq[:, 4 * g2 + cc, 0:SB], start=(cc == 0), stop=(cc == 3)),
                                reads=[B_gsq, B_const], writes=[B_bank[5]], sig=(cc == 3))
                  em.op('act', lambda e: e.activation(out=rstd[:, :, 0:SB], in_=psR[:, 0:2 * SB].rearrange("p (g q) -> p g q", g=2),
                                                      func=AF.Sqrt, scale=1.0 / 512.0, bias=RMS_EPS),
                        reads=[B_bank[5]], writes=[B_rstd])
                  em.op('dve', lambda e: e.reciprocal(out=rstd[:, :, 0:SB], in_=rstd[:, :, 0:SB]), reads=[B_rstd], writes=[B_rstd])
                  em.op('dve', lambda e: e.tensor_tensor(
                      out=t1[:, :, 0:SB].rearrange("p (g c) q -> p g c q", g=2),
                      in0=gT[:, :, 0:SB].rearrange("p (g c) q -> p g c q", g=2),
                      in1=rstd[:, :, 0:SB].unsqueeze(2).broadcast_to([128, 2, 4, SB]), op=ALU.mult),
                      reads=[B_gT, B_rstd], writes=[B_t1])
                  em.op('pool', lambda e: e.tensor_tensor(out=mixT[:, 0:8, cols], in0=t1[:, :, 0:SB],
                                                          in1=normw[:, :].unsqueeze(2).broadcast_to([128, 8, SB]), op=ALU.mult),
                        reads=[B_t1, B_par], writes=B_mix[0:8])
                  yield

            ssd_state = {"gen": None}

            slot_ctr = [0]

            def ssd_slot():
                slot_ctr[0] += 1
                if slot_ctr[0] % 2 == 1:
                    ssd_step()

            def ssd_step():
                if ssd_state["gen"] is None:
                    ssd_state["gen"] = ssd_gen()
                try:
                    next(ssd_state["gen"])
                    return True
                except StopIteration:
                    return False

            if not w_pref[0]:
                load_group(0)
            for gi, (c0, kind, gidx) in enumerate(groups):
                _ck(3.0 + 0.01 * gi)
                if gi + 1 < len(groups) and not (gi == 0 and w_pref[0]):
                    load_group(gi + 1)
                if pieces:
                    pieces.pop(0)()
                if gi >= 6 and pieces:
                    pieces.pop(0)()
                s = gi % 2
                if kind == 'v':
                    hh = gidx
                    for j in range(nsub):
                        bi = (4, 7)[j % 2] if gi >= SSD_GI else 2 + (j % 2)
                        psv = bank(bi)
                        for kc in range(KC):
                            em.op('pe', lambda e, j=j, kc=kc, psv=psv, s=s: e.matmul(
                                psv[0:SB, 0:512], lhsT=xT[:, kc, j * SB:(j + 1) * SB], rhs=wsl[:, s, kc, :],
                                start=(kc == 0), stop=(kc == KC - 1)),
                                reads=[B_xT, B_wsl[s]], writes=[B_bank[bi]], sig=(kc == KC - 1))
                        sl = (hh * nsub + j) % 2
                        vfa, B_vf = vf[sl]
                        vba, B_vb = vb[sl]
                        em.op('act', lambda e, psv=psv, vfa=vfa: e.activation(out=vfa[0:SB, :], in_=psv[0:SB, 0:512], func=AF.Copy),
                              reads=[B_bank[bi]], writes=[B_vf])
                        em.op('dve', lambda e, vfa=vfa, vba=vba: e.tensor_copy(
                            out=vba[0:SB, :, 0:64], in_=vfa[0:SB, :].rearrange("p (h d) -> p h d", h=8)),
                            reads=[B_vf], writes=[B_vb])
                        em.dma("vo%d" % sl, v_o[t0 + j * SB:t0 + (j + 1) * SB, hh * 512:(hh + 1) * 512], vfa[0:SB, :], reads=[B_vf])
                        em.dma("vs%d" % sl, vscr[4 * hh:4 * hh + 4, 0:SB, kb0 + j, :].rearrange("c p d -> p c d"),
                               vba[0:SB, :, :].rearrange("p (c t) d -> p c (t d)", t=2), reads=[B_vb], writes=[("vscr", 4 * hh + i) for i in range(4)])
                        if gi >= SSD_GI:
                            ssd_slot()
                    continue
                for cc in range(4):
                    ch = gidx * 4 + cc
                    if gi >= SSD_GI:
                        bi = (4, 7)[fm_ctr[0] % 2]
                        fm_ctr[0] += 1
                    else:
                        bi = fm_ctr[0] % 2
                        fm_ctr[0] += 1
                    ps = bank(bi)
                    for kc in range(KC):
                        em.op('pe', lambda e, kc=kc, ps=ps, s=s, cc=cc: e.matmul(
                            ps[:, 0:TBK], lhsT=wsl[:, s, kc, 128 * cc:128 * cc + 128], rhs=xT[:, kc, 0:TBK],
                            start=(kc == 0), stop=(kc == KC - 1)),
                            reads=[B_xT, B_wsl[s]], writes=[B_bank[bi]], sig=(kc == KC - 1))
                    Bb = B_bank[bi]
                    if ch < 8:
                        c = ch
                        em.op('act', lambda e, ps=ps, c=c: e.activation(out=zsT[:, c, 0:TBK], in_=ps[:, 0:TBK], func=AF.Silu),
                              reads=[Bb], writes=[B_zsT[c]])
                    elif ch < 20:
                        c = ch - 8
                        sl = c % 2
                        csa, B_cs = cst[sl]
                        caa, B_ca = cacc[sl]
                        em.op('act', lambda e, ps=ps, csa=csa: e.activation(out=csa[:, 3:3 + TBK], in_=ps[:, 0:TBK], func=AF.Copy),
                              reads=[Bb], writes=[B_cs])
                        em.op('pool', lambda e, csa=csa, c=c: e.tensor_copy(out=csa[:, 0:3], in_=ccar[:, c, :]),
                              reads=[B_ccar[c]], writes=[B_cs])
                        em.op('pool', lambda e, csa=csa, c=c: e.tensor_copy(out=ccar[:, c, :], in_=csa[:, TBK:TBK + 3]),
                              reads=[B_cs], writes=[B_ccar[c]])
                        if last_block:
                            em.dma("cvo", conv_o[:, c, :], ccar[:, c, :], reads=[B_ccar[c]])
                        em.op('act', lambda e, ps=ps, caa=caa, c=c: e.activation(
                            out=caa[:, 0:TBK], in_=ps[:, 0:TBK], func=AF.Copy, scale=convw[:, c, 3:4]),
                            reads=[Bb, B_par], writes=[B_ca])
                        for tap in range(0, 3):
                            eng = 'dve'
                            em.op(eng, lambda e, csa=csa, caa=caa, c=c, tap=tap: e.scalar_tensor_tensor(
                                out=caa[:, 0:TBK], in0=csa[:, tap:tap + TBK], scalar=convw[:, c, tap:tap + 1], in1=caa[:, 0:TBK],
                                op0=ALU.mult, op1=ALU.add), reads=[B_cs, B_ca, B_par], writes=[B_ca])
                        if c < 8:
                            dst, Bd = xcT[:, c, 0:TBK], B_xcT[c]
                        else:
                            dst, Bd = BCT[:, c - 8, 0:TBK], B_BCT[c - 8]
                        em.op('act', lambda e, caa=caa, c=c, dst=dst: e.activation(out=dst, in_=caa[:, 0:TBK], func=AF.Silu,
                                                                                   bias=convb[:, c:c + 1]),
                              reads=[B_ca, B_par], writes=[Bd])
                    elif ch < 28:
                        c = ch - 20
                        for hp in range(2):
                            em.op('dve', lambda e, ps=ps, c=c, hp=hp: e.tensor_scalar(
                                out=qA[0:64, 2 * c + hp, 0:TBK], in0=ps[64 * hp:64 * hp + 64, 0:TBK], scalar1=0.125,
                                scalar2=None, op0=ALU.mult), reads=[Bb], writes=[B_qA[2 * c + hp]])
                    elif ch < 36:
                        c = ch - 28
                        sl = c % 2
                        kfa, B_kf = kf[sl]
                        em.op('act', lambda e, ps=ps, kfa=kfa: e.activation(out=kfa[:, 0:TBK], in_=ps[:, 0:TBK], func=AF.Copy),
                              reads=[Bb], writes=[B_kf])
                        em.op('dve', lambda e, kfa=kfa, c=c: e.tensor_copy(out=kTc[:, c, 0:TBK], in_=kfa[:, 0:TBK]),
                              reads=[B_kf], writes=[B_kTc[c]])
                        em.dma("ko%d" % sl, k_o[c, :, t0:t0 + TBK], kfa[:, 0:TBK], reads=[B_kf])
                        for hp in range(2):
                            em.dma("ks%d" % (2 * c + hp), kscr[2 * c + hp, 0:64, key0:key0 + TBK], kTc[64 * hp:64 * hp + 64, c, 0:TBK],
                                   reads=[B_kTc[c]], writes=[("kscr", 2 * c + hp)])
                    else:
                        c = ch - 36
                        em.op('act', lambda e, ps=ps, c=c: e.activation(out=zaT[:, c, 0:TBK], in_=ps[:, 0:TBK], func=AF.Silu),
                              reads=[Bb], writes=[B_zaT[c]])
                    if gi >= SSD_GI:
                        ssd_slot()

            while pieces:
                pieces.pop(0)()
            _ck(4)
            w_pref[0] = nxt is not None
            if w_pref[0]:
                load_group(0)
                load_group(1)
            while ssd_step():
                pass
            _ck(5)
            bba, B_bb = Bj[0]
            em.op('dve', lambda e: e.tensor_tensor(
                out=bba[:, 0:nkb, :], in0=Fblk0[:, :].unsqueeze(1).broadcast_to([128, nkb, 16]), in1=Fall[:, 0:nkb, :],
                op=ALU.subtract), reads=[B_Fblk0] + B_Fall[0:nkb], writes=[B_bb])

            def load_k(h):
                s_ = h % 2
                em.dma("kl%d" % s_, ktS[s_][0][0:66, 0:nkeys], kscr[h, :, 0:nkeys], reads=[("kscr", h)], writes=[ktS[s_][1]])

            def load_v(c):
                s_ = c % 2
                em.dma("vl%d" % s_, vS[s_][0][:, 0:nkb, :], vscr[c, :, 0:nkb, :], reads=[("vscr", c)], writes=[vS[s_][1]])

            S_banks = [0, 1, 4, 6, 7]
            items = [(h, kb) for h in range(16) for kb in range(nkb)]
            LOOK = 4

            def geom(kb):
                if kb < kb0:
                    return 128, 0
                return SB, kb - kb0

            def emit_qk(idx):
                h, kb = items[idx]
                kta, B_kt = ktS[h % 2]
                ksz, i0_ = geom(kb)
                c0 = i0_ * SB
                ko = koff(kb)
                sbk = S_banks[idx % 5]
                ps = bank(sbk)
                diag = kb >= kb0
                em.op('pe', lambda e: e.matmul(ps[0:ksz, c0:TBK], lhsT=kta[0:66, ko:ko + ksz], rhs=qA[0:66, h, c0:TBK],
                                               start=True, stop=(not diag)),
                      reads=[B_kt, B_qA[h]], writes=[B_bank[sbk]], sig=(not diag))
                if diag:
                    em.op('pe', lambda e: e.matmul(ps[0:SB, c0:c0 + SB], lhsT=identb[0:SB, 0:SB], rhs=negmask[0:SB, 0:SB],
                                                   start=False, stop=True),
                          reads=[B_identb, B_const], writes=[B_bank[sbk]])
                if kb == nkb - 1 and h + 2 < 16:
                    load_k(h + 2)

            def emit_exp(idx):
                h, kb = items[idx]
                ksz, i0_ = geom(kb)
                c0 = i0_ * SB
                sbk = S_banks[idx % 5]
                ps = bank(sbk)
                pts, B_pt = PTs[idx % 4]
                em.op('act', lambda e: e.activation(out=pts[0:ksz, c0:TBK], in_=ps[0:ksz, c0:TBK], func=AF.Exp,
                                                    bias=bba[0:ksz, kb, h:h + 1]),
                      reads=[B_bank[sbk], B_bb], writes=[B_pt])

            def emit_pv(idx):
                h, kb = items[idx]
                c, hp = h // 2, h % 2
                vsa, B_vs = vS[c % 2]
                ksz, i0_ = geom(kb)
                c0 = i0_ * SB
                pts, B_pt = PTs[idx % 4]
                ob = 2 + (h % 2)
                psO = bank(ob)
                em.op('pe', lambda e: e.matmul(
                    psO[0:65, c0:TBK], lhsT=vsa[0:ksz, kb, 65 * hp:65 * hp + 65], rhs=pts[0:ksz, c0:TBK],
                    start=(kb == 0), stop=(kb == nkb - 1)), reads=[B_pt, B_vs], writes=[B_bank[ob]])
                if kb == nkb - 1:
                    prt = slice(64 * hp, 64 * hp + 64)
                    rl, B_rl = rl2[h % 2]
                    rlb, B_rlb = rlb2[h % 2]
                    onb, B_on = onb2[h % 2]
                    em.op('dve', lambda e: e.reciprocal(out=rl[64:65, 0:TBK], in_=psO[64:65, 0:TBK]), reads=[B_bank[ob]], writes=[B_rl])
                    if hp == 1 and c + 2 < 8:
                        load_v(c + 2)
                    psL = bank(5)

                    def part2a():
                        em.op('pe', lambda e: e.matmul(psL[0:64, 0:TBK], lhsT=ones_f[64:65, 0:64], rhs=rl[64:65, 0:TBK], start=True, stop=True),
                              reads=[B_rl, B_const], writes=[B_bank[5]])

                    def part2b():
                        em.op('act', lambda e: e.activation(out=rlb[0:64, 0:TBK], in_=psL[0:64, 0:TBK], func=AF.Copy),
                              reads=[B_bank[5]], writes=[B_rlb])
                        em.op('dve', lambda e: e.tensor_tensor(out=onb[prt, 0:TBK], in0=psO[0:64, 0:TBK], in1=rlb[0:64, 0:TBK], op=ALU.mult),
                              reads=[B_bank[ob], B_rlb], writes=[B_on])
                        em.op('pool', lambda e: e.tensor_tensor(out=mixT[prt, 8 + c, 0:TBK], in0=onb[prt, 0:TBK],
                                                                in1=zaT[prt, c, 0:TBK], op=ALU.mult),
                              reads=[B_on, B_zaT[c]], writes=[B_mix[8 + c]])
                    deferred.append((idx + DEFER, part2a))
                    deferred.append((idx + DEFER + 2, part2b))

            load_k(0)
            load_k(1)
            load_v(0)
            load_v(1)
            nit = len(items)
            DEFER = max(1, min(8, nkb - 2))
            deferred = []
            for idx in range(min(LOOK, nit)):
                emit_qk(idx)
            for idx in range(nit):
                emit_exp(idx)
                if idx + LOOK < nit:
                    emit_qk(idx + LOOK)
                deferred.sort(key=lambda t: t[0])
                while deferred and deferred[0][0] <= idx:
                    deferred.pop(0)[1]()
                emit_pv(idx)
            while deferred:
                deferred.pop(0)[1]()

            if nxt is not None:
                prefetch_x(*nxt)
            _ck(6)
            def load_xr(j_):
                s_ = j_ % 2
                em.dma("xr%d" % s_, xres[s_][0][0:SB, :], x_src[t0 + j_ * SB:t0 + (j_ + 1) * SB, :], writes=[xres[s_][1]])
            for j_ in range(min(2, nsub)):
                load_xr(j_)
            for j in range(nsub):
                s = j % 2
                xra, B_xr = xres[s]
                hba, B_hb = hb[s]
                lsa, B_ls = lnst[s]
                psOut = PS[2 + s]
                for half in range(2):
                    bi = 4 + 2 * s + half
                    for fc in range(16):
                        em.op('pe', lambda e, fc=fc, half=half, psOut=psOut, j=j: e.matmul(
                            psOut[0:SB, 512 * half:512 * half + 512], lhsT=mixT[:, fc, j * SB:(j + 1) * SB],
                            rhs=wo[:, fc, 512 * half:512 * half + 512], start=(fc == 0), stop=(fc == 15)),
                            reads=[B_mix[fc], B_wo[fc // 4]], writes=[B_bank[bi]], sig=(fc == 15))
                em.op('dve', lambda e, xra=xra, hba=hba, psOut=psOut: e.scalar_tensor_tensor(
                    out=hba[0:SB, :], in0=xra[0:SB, :], scalar=ALPHA, in1=psOut[0:SB, :], op0=ALU.mult, op1=ALU.add),
                    reads=[B_xr, B_bank[4 + 2 * s], B_bank[5 + 2 * s]], writes=[B_hb])
                if j + 2 < nsub:
                    load_xr(j + 2)
                for half in range(2):
                    em.op('dve', lambda e, hba=hba, lsa=lsa, half=half: e.bn_stats(out=lsa[0:SB, 6 * half:6 * half + 6],
                                                                                  in_=hba[0:SB, 512 * half:512 * half + 512]),
                          reads=[B_hb], writes=[B_ls])
                em.op('dve', lambda e, lsa=lsa: e.bn_aggr(out=lsa[0:SB, 12:14], in_=lsa[0:SB, 0:12]), reads=[B_ls], writes=[B_ls])
                em.op('act', lambda e, lsa=lsa: e.activation(out=lsa[0:SB, 14:15], in_=lsa[0:SB, 13:14], func=AF.Sqrt, bias=LN_EPS),
                      reads=[B_ls], writes=[B_ls])
                em.op('dve', lambda e, lsa=lsa: e.reciprocal(out=lsa[0:SB, 14:15], in_=lsa[0:SB, 14:15]), reads=[B_ls], writes=[B_ls])
                em.op('dve', lambda e, hba=hba, lsa=lsa: e.tensor_scalar(
                    out=hba[0:SB, :], in0=hba[0:SB, :], scalar1=lsa[0:SB, 12:13], scalar2=lsa[0:SB, 14:15],
                    op0=ALU.subtract, op1=ALU.mult), reads=[B_hb, B_ls], writes=[B_hb])
                em.op('pool', lambda e, hba=hba: e.tensor_tensor(out=hba[0:SB, :], in0=hba[0:SB, :], in1=lng[0:SB, :], op=ALU.mult),
                      reads=[B_hb, B_par], writes=[B_hb])
                em.op('pool', lambda e, hba=hba: e.tensor_tensor(out=hba[0:SB, :], in0=hba[0:SB, :], in1=lnb[0:SB, :], op=ALU.add),
                      reads=[B_hb, B_par], writes=[B_hb])
                em.dma("yo%d" % s, y_o[t0 + j * SB:t0 + (j + 1) * SB, :], hba[0:SB, :], reads=[B_hb])

        def init_state_zero():
            _ck(2.5)
            em.op('pool', lambda e: e.memset(hT[:], 0.0), writes=[B_hT])
            em.op('pool', lambda e: e.memset(hTb[:], 0.0), writes=[B_hTb])
            em.op('pool', lambda e: e.memset(ccar[:], 0.0), writes=B_ccar)
            em.op('pool', lambda e: e.memset(Fcar[:], 0.0), writes=[B_Fcar])

        for sq in range(NSEQ):
            init_state_zero()
            outs = (y_p[sq], k_p[sq], v_p[sq], lf_p[sq], conv_p[sq])
            nblk = T // TB
            for bi in range(nblk):
                if bi + 1 < nblk:
                    nxt = (xT_p[sq], (bi + 1) * TB, TB)
                elif sq + 1 < NSEQ:
                    nxt = (xT_p[sq + 1], 0, TB)
                else:
                    nxt = (xT_s, 0, TS)
                do_block(xT_p[sq], x_p[sq], bi * TB, TB, 128, 0, outs, bi == nblk - 1, first=(sq == 0 and bi == 0), nxt=nxt)
            em.dma("sso", ssm_p[sq], hT[:, :], reads=[B_hT])

        em.op('pool', lambda e: e.memset(Fcar[:], 0.0), writes=[B_Fcar])
        em.dma("sti_h", hT[:, :], sssm[:, :], writes=[B_hT])
        em.op('act', lambda e: e.activation(out=hTb[:, :], in_=hT[:, :], func=AF.Copy), reads=[B_hT], writes=[B_hTb])
        em.dma("sti_c", ccar[:], sconv[:, :, :], writes=B_ccar)
        for r in range(0, PAST, 512):
            w = min(512, PAST - r)
            em.dma("xT", xTf[:, :, 0:w], ckT[:, :, r:r + w].rearrange("c p t -> p c t"), writes=[B_xTf])
            em.op('dve', lambda e, w=w: e.tensor_copy(out=kTc[:, 0:4, 0:w], in_=xTf[:, 0:4, 0:w]), reads=[B_xTf], writes=B_kTc[0:4])
            em.op('pool', lambda e, w=w: e.tensor_copy(out=kTc[:, 4:8, 0:w], in_=xTf[:, 4:8, 0:w]), reads=[B_xTf], writes=B_kTc[4:8])
            for c in range(8):
                for hp in range(2):
                    em.dma("ks%d" % (2 * c + hp), kscr[2 * c + hp, 0:64, r:r + w], kTc[64 * hp:64 * hp + 64, c, 0:w],
                           reads=[B_kTc[c]], writes=[("kscr", 2 * c + hp)])
        for pb in range(PAST // 128):
            for hh in range(2):
                sl = hh
                vfa, B_vf = vf[sl]
                vba, B_vb = vb[sl]
                em.dma("vo%d" % sl, vfa[:, :], cv[pb * 128:(pb + 1) * 128, hh * 512:(hh + 1) * 512], writes=[B_vf])
                em.op('dve' if hh == 0 else 'pool', lambda e, vfa=vfa, vba=vba: e.tensor_copy(
                    out=vba[:, :, 0:64], in_=vfa[:, :].rearrange("p (h d) -> p h d", h=8)), reads=[B_vf], writes=[B_vb])
                em.dma("vs%d" % sl, vscr[4 * hh:4 * hh + 4, :, pb, :].rearrange("c p d -> p c d"),
                       vba[:, :, :].rearrange("p (c t) d -> p c (t d)", t=2), reads=[B_vb], writes=[("vscr", 4 * hh + i) for i in range(4)])
            j = pb % 4
            em.dma("lfi%d" % j, lft[:, j, :], clf[pb * 128:(pb + 1) * 128, :], writes=[B_lft[j]])
            f_update(lft[:, j, :], B_lft[j], 128, pb, j)
        do_block(xT_s, x_s, 0, TS, TS, PAST, (y_s, k_s, v_s, lf_s, conv_s), True)
        em.dma("sso", ssm_s, hT[:, :], reads=[B_hT])

        with nc.Block() as block:
            em.replay(block)
    return nc


_CACHE = {}


def _get_prog(NSEQ, T, PAST, TS):
    key = (NSEQ, T, PAST, TS)
    if key not in _CACHE:
        _CACHE[key] = build_program(NSEQ, T, PAST, TS)
    return _CACHE[key]


def kernel(x_prompt, x_sample, cache_k, cache_v, cache_logf, state_conv, state_ssm,
           w_in, conv_w, conv_b, dt_bias, a_log, d_skip, ssm_norm_w, f_bias, w_out, ln_g, ln_b):
    f32 = np.float32
    x_prompt = np.asarray(x_prompt, f32); x_sample = np.asarray(x_sample, f32)
    B, T, _ = x_prompt.shape
    DB, TS, _ = x_sample.shape
    PAST = cache_k.shape[2]
    assert B % N_CORES == 0 and DB == N_CORES
    NSEQ = B // N_CORES
    nc = _get_prog(NSEQ, T, PAST, TS)

    w_in0 = np.asarray(w_in, f32)[0]
    shared = {
        "w_in": np.ascontiguousarray(w_in0.reshape(KC, 128, DIN)),
        "w_out": np.ascontiguousarray(np.asarray(w_out, f32)[0].reshape(16, 128, D)),
        "wdtf_f": np.ascontiguousarray(np.concatenate([w_in0[:, DT0:DT0 + 16], w_in0[:, F0:F0 + 16]], axis=1).reshape(KC, 128, 32)),
        "convw": np.ascontiguousarray(np.asarray(conv_w, f32)[0].T.reshape(12, 128, 4).transpose(1, 0, 2)),
        "convb": np.ascontiguousarray(np.asarray(conv_b, f32)[0].reshape(12, 128).T),
        "dtb": np.ascontiguousarray(np.asarray(dt_bias, f32)[0]),
        "alog": np.ascontiguousarray(np.asarray(a_log, f32)[0]),
        "fb": np.ascontiguousarray(np.asarray(f_bias, f32)[0]),
        "dcol": np.ascontiguousarray(np.repeat(np.asarray(d_skip, f32)[0].reshape(8, 2), 64, axis=1).T),
        "normw": np.ascontiguousarray(np.asarray(ssm_norm_w, f32)[0].reshape(8, 128).T),
        "lng": np.ascontiguousarray(np.asarray(ln_g, f32)[0]),
        "lnb": np.ascontiguousarray(np.asarray(ln_b, f32)[0]),
    }
    ck = np.asarray(cache_k, f32)[0]; cvv = np.asarray(cache_v, f32)[0]; cl = np.asarray(cache_logf, f32)[0]
    sc = np.asarray(state_conv, f32)[0]; ss = np.asarray(state_ssm, f32)[0]
    in_maps = []
    for c in range(N_CORES):
        xp = x_prompt[c * NSEQ:(c + 1) * NSEQ]
        m = dict(shared)
        m["x_p"] = np.ascontiguousarray(xp)
        m["xT_p"] = np.ascontiguousarray(xp.transpose(0, 2, 1))
        m["x_s"] = np.ascontiguousarray(x_sample[c])
        m["xT_s"] = np.ascontiguousarray(x_sample[c].T)
        m["ckT"] = np.ascontiguousarray(ck[c].reshape(PAST, D).T.reshape(8, 128, PAST))
        m["cv"] = np.ascontiguousarray(cvv[c].reshape(PAST, D))
        m["clf"] = np.ascontiguousarray(cl[c])
        m["sconv"] = np.ascontiguousarray(sc[c].T.reshape(12, 128, 3).transpose(1, 0, 2))
        m["sssm"] = np.ascontiguousarray(ss[c].reshape(D, 128).T)
        in_maps.append(m)
    res = run_bass_kernel_spmd(nc, in_maps, core_ids=list(range(N_CORES)))
    R = res.results

    def cat(name):
        return np.concatenate([np.asarray(r[name]) for r in R], axis=0)

    def stack(name):
        return np.stack([np.asarray(r[name]) for r in R], axis=0)

    y_prompt = cat("y_p")
    y_sample = stack("y_s")
    k_prompt = cat("k_p").reshape(B, D, T).transpose(0, 2, 1).reshape(1, B, T, 16, 64)
    v_prompt = cat("v_p").reshape(1, B, T, 16, 64)
    logf_prompt = cat("lf_p").reshape(1, B, T, 16)
    conv_prompt = cat("conv_p").transpose(0, 2, 1, 3).reshape(B, 1536, 3).transpose(0, 2, 1).reshape(1, B, 3, 1536)
    ssm_prompt = cat("ssm_p").transpose(0, 2, 1).reshape(1, B, 16, 64, 128)
    k_sample = stack("k_s").reshape(DB, D, TS).transpose(0, 2, 1).reshape(1, DB, TS, 16, 64)
    v_sample = stack("v_s").reshape(1, DB, TS, 16, 64)
    logf_sample = stack("lf_s").reshape(1, DB, TS, 16)
    conv_sample = stack("conv_s").transpose(0, 2, 1, 3).reshape(DB, 1536, 3).transpose(0, 2, 1).reshape(1, DB, 3, 1536)
    ssm_sample = stack("ssm_s").transpose(0, 2, 1).reshape(1, DB, 16, 64, 128)
    outs = (y_prompt, y_sample, k_prompt, v_prompt, logf_prompt, conv_prompt, ssm_prompt,
            k_sample, v_sample, logf_sample, conv_sample, ssm_sample)
    return tuple(np.ascontiguousarray(o, dtype=f32) for o in outs)
```

```python
import numpy as np
from contextlib import ExitStack
import concourse.bass as bass
import concourse.mybir as mybir
from concourse.bass_utils import run_bass_kernel_spmd

F32 = mybir.dt.float32
BF16 = mybir.dt.bfloat16
AF = mybir.ActivationFunctionType
ALU = mybir.AluOpType

D = 1024
KC = 8
DIN = 6688
ZS0, XBC0, DT0, Q0, K0, V0, ZA0, F0 = 0, 1024, 2560, 2576, 3600, 4624, 5648, 6672
ALPHA = 2.0 ** 0.25
LN_EPS = 1e-5
RMS_EPS = 1e-5
N_CORES = 8
import os as _os
KSTAGE = float(_os.environ.get("KSTAGE", "99"))


KVAR = int(_os.environ.get("KVAR", "3"))


class _Stop(Exception):
    pass


class Buf:
    def __init__(self, keys):
        self.keys = list(keys)


class _Rec:
    def __getattr__(self, name):
        return lambda *a, **k: (name, a, k)


_REC = _Rec()


class Em:
    CE = ['pe', 'act', 'dve', 'pool']

    def __init__(self, nc, es):
        self.nc = nc
        self.es = es
        self.ops = {e: [] for e in self.CE + ['sp']}
        self.sem = {e: es.enter_context(nc.semaphore("s_" + e)) for e in self.CE}
        self.seq = {e: 0 for e in self.CE}
        self.flag = {e: [] for e in self.CE}
        self.waited = {}
        self.lastw = {}
        self.readers = {}
        self.chan = {}

    def _keys(self, lst):
        out = []
        for b in lst:
            if isinstance(b, Buf):
                out.extend(b.keys)
            elif isinstance(b, (list, tuple)) and len(b) > 0 and isinstance(b[0], Buf):
                for x in b:
                    out.extend(x.keys)
            else:
                out.append(b)
        return out

    def _resolve(self, d):
        if d[0] == 'e':
            _, eng, seq = d
            fl = self.flag[eng]
            lo, hi = 0, len(fl)
            while lo < hi:
                mid = (lo + hi) // 2
                if fl[mid] >= seq:
                    hi = mid
                else:
                    lo = mid + 1
            assert lo < len(fl), f"dep on unflagged {eng} op {seq} with no later flagged op"
            return ('e', eng), self.sem[eng], lo + 1
        else:
            _, ch, cnt = d
            return ('d', ch), self.chan[ch][0], 16 * self.chan[ch][1]

    def _deps(self, eng, reads, writes):
        deps = []
        for k in reads:
            d = self.lastw.get(k)
            if d is not None:
                deps.append(d)
        for k in writes:
            d = self.lastw.get(k)
            if d is not None:
                deps.append(d)
            for r in self.readers.get(k, {}).values():
                deps.append(r)
        best = {}
        for d in deps:
            if d[0] == 'e' and d[1] == eng and eng == 'pe':
                continue
            src, sem, val = self._resolve(d)
            if self.waited.get((eng, src), 0) >= val:
                continue
            if src not in best or best[src][1] < val:
                best[src] = (sem, val)
        waits = []
        for src, (sem, val) in best.items():
            self.waited[(eng, src)] = val
            waits.append((sem, val))
        return waits

    def _record(self, me, srckey, reads, writes):
        for k in reads:
            self.readers.setdefault(k, {})[srckey] = me
        for k in writes:
            self.lastw[k] = me
            self.readers[k] = {}

    stopped = False

    def op(self, eng, fn, reads=(), writes=(), sig=True):
        if self.stopped:
            return
        reads = self._keys(reads)
        writes = self._keys(writes)
        waits = self._deps(eng, reads, writes)
        seq = self.seq[eng]
        self.seq[eng] += 1
        if sig:
            self.flag[eng].append(seq)
        name, a, k = fn(_REC)
        self.ops[eng].append((waits, (lambda e, name=name, a=a, k=k: getattr(e, name)(*a, **k)), sig))
        self._record(('e', eng, seq), eng, reads, writes)

    def dma(self, ch, out, in_, reads=(), writes=(), **kw):
        if self.stopped:
            return
        if ch not in self.chan:
            self.chan[ch] = [self.es.enter_context(self.nc.semaphore("d_" + ch)), 0]
        reads = self._keys(reads)
        writes = self._keys(writes)
        waits = self._deps('sp', reads, writes)
        c = self.chan[ch]
        c[1] += 1
        sem = c[0]

        def fn(e, out=out, in_=in_, sem=sem, kw=kw):
            return e.dma_start(out=out, in_=in_, **kw).then_inc(sem, 16)
        self.ops['sp'].append((waits, fn, False))
        self._record(('d', ch, c[1]), ('d', ch), reads, writes)

    def replay(self, block):
        em = self

        def run(eng_name):
            def body(e):
                sem = em.sem.get(eng_name)
                for waits, fn, sig in em.ops[eng_name]:
                    for s, v in waits:
                        e.wait_ge(s, v)
                    ins = fn(e)
                    if sig:
                        ins.then_inc(sem, 1)
                if eng_name == 'sp':
                    for ch, (s, cnt) in em.chan.items():
                        if cnt > 0:
                            e.wait_ge(s, 16 * cnt)
                    for ce in em.CE:
                        n = len(em.flag[ce])
                        if n > 0:
                            e.wait_ge(em.sem[ce], n)
            return body
        block.tensor(run('pe'))
        block.scalar(run('act'))
        block.vector(run('dve'))
        block.gpsimd(run('pool'))
        block.sync(run('sp'))


def build_program(NSEQ, T, PAST, TS):
    TB = 512
    assert T % TB == 0
    KMAX = ((max(T, PAST + TS) + 511) // 512) * 512
    NKB = max(T // 128, PAST // 128 + 1)
    nc = bass.Bass("TRN2", target_bir_lowering=False)

    def din(name, shape):
        return nc.dram_tensor(name, list(shape), F32, kind="ExternalInput").ap()

    def dout(name, shape):
        return nc.dram_tensor(name, list(shape), F32, kind="ExternalOutput").ap()

    xT_p = din("xT_p", [NSEQ, D, T]); x_p = din("x_p", [NSEQ, T, D])
    xT_s = din("xT_s", [D, TS]); x_s = din("x_s", [TS, D])
    ckT = din("ckT", [8, 128, PAST]); cv = din("cv", [PAST, D]); clf = din("clf", [PAST, 16])
    sconv = din("sconv", [128, 12, 3]); sssm = din("sssm", [128, D])
    w_in = din("w_in", [KC, 128, DIN]); w_out = din("w_out", [16, 128, D]); wdtf_f = din("wdtf_f", [KC, 128, 32])
    convw_d = din("convw", [128, 12, 4]); convb_d = din("convb", [128, 12])
    dtb_d = din("dtb", [16]); alog_d = din("alog", [16]); fb_d = din("fb", [16])
    dcol_d = din("dcol", [128, 8]); normw_d = din("normw", [128, 8])
    lng_d = din("lng", [D]); lnb_d = din("lnb", [D])

    y_p = dout("y_p", [NSEQ, T, D]); y_s = dout("y_s", [TS, D])
    k_p = dout("k_p", [NSEQ, 8, 128, T]); v_p = dout("v_p", [NSEQ, T, D]); lf_p = dout("lf_p", [NSEQ, T, 16])
    conv_p = dout("conv_p", [NSEQ, 128, 12, 3]); ssm_p = dout("ssm_p", [NSEQ, 128, D])
    k_s = dout("k_s", [8, 128, TS]); v_s = dout("v_s", [TS, D]); lf_s = dout("lf_s", [TS, 16])
    conv_s = dout("conv_s", [128, 12, 3]); ssm_s = dout("ssm_s", [128, D])

    GROUP_COLS = [512 * g for g in range(5)] + [K0, K0 + 512, V0, V0 + 512, Q0, Q0 + 512, ZA0, ZA0 + 512]
    wbf = nc.dram_tensor("wbf", [13, 128, KC * 512], BF16, kind="Internal").ap()
    kscr = nc.dram_tensor("kscr", [16, 66, KMAX], BF16, kind="Internal").ap()
    vscr = nc.dram_tensor("vscr", [8, 128, NKB, 130], BF16, kind="Internal").ap()

    with ExitStack() as es:
        em = Em(nc, es)

        def sb(name, shape, dt):
            return es.enter_context(nc.sbuf_tensor(name, list(shape), dt))

        identb = sb("identb", [128, 128], BF16); B_identb = Buf(["identb"])
        tri = sb("tri", [128, 128], F32)
        ustr = sb("ustr", [128, 128], F32)
        ones_f = sb("ones_f", [128, 128], F32)
        ones_b = sb("ones_b", [128, 128], BF16)
        cmaskb = sb("cmaskb", [128, 128], BF16)
        negmask = sb("negmask", [128, 128], BF16)
        B_const = Buf(["consts"])
        wdtf = sb("wdtf", [128, KC, 32], BF16); B_wdtf = Buf(["wdtf"])
        convw = sb("convw_t", [128, 12, 4], F32); convb = sb("convb_t", [128, 12], F32)
        dcol = sb("dcol_t", [128, 8], F32); normw = sb("normw_t", [128, 8], F32)
        dtb = sb("dtb_t", [128, 16], F32); Abc = sb("Abc", [128, 16], F32); fbb = sb("fbb", [128, 16], F32)
        lng = sb("lng_t", [128, D], F32); lnb = sb("lnb_t", [128, D], F32)
        B_par = Buf(["params"])
        wo = sb("wo", [128, 16, D], BF16); B_wo = [Buf([("wo", i)]) for i in range(4)]
        hT = sb("hT", [128, D], F32); B_hT = Buf(["hT"])
        hTb = sb("hTb", [128, D], BF16); B_hTb = Buf(["hTb"])
        ccar = sb("ccar", [128, 12, 3], F32); B_ccar = [Buf([("ccar", c)]) for c in range(12)]
        Fcar = sb("Fcar", [128, 16], F32); B_Fcar = Buf(["Fcar"])
        Fall = sb("Fall", [128, NKB, 16], F32); B_Fall = [Buf([("Fall", i)]) for i in range(NKB)]
        Fend = sb("Fend", [128, 4, 16], F32); B_Fend = [Buf([("Fend", i)]) for i in range(4)]

        xT = sb("xT", [128, KC, TB], BF16); B_xT = Buf(["xT"])
        wsl = sb("wsl", [128, 2, KC, 512], BF16); B_wsl = [Buf([("wsl", i)]) for i in range(2)]
        qA = sb("qA", [128, 16, TB], BF16); B_qA = [Buf([("qA", h)]) for h in range(16)]
        onesrow = sb("onesrow", [128, 512], BF16)
        dhl = sb("dhl", [128, 4, 16, 2], BF16); B_dhl = [Buf([("dhl", j)]) for j in range(4)]
        dlt = sb("dlt", [128, 4, 16], F32); B_dlt = [Buf([("dlt", j)]) for j in range(4)]
        Fblk0 = sb("Fblk0", [128, 16], F32); B_Fblk0 = Buf(["Fblk0"])
        zsT = sb("zsT", [128, 8, TB], BF16); B_zsT = [Buf([("zsT", c)]) for c in range(8)]
        zaT = sb("zaT", [128, 8, TB], BF16); B_zaT = [Buf([("zaT", c)]) for c in range(8)]
        kTc = zaT; B_kTc = B_zaT
        xcT = sb("xcT", [128, 8, TB], BF16); B_xcT = [Buf([("xcT", c)]) for c in range(8)]
        BCT = sb("BCT", [128, 4, TB], BF16); B_BCT = [Buf([("BCT", c)]) for c in range(4)]
        mixT = sb("mixT", [128, 16, TB], BF16); B_mix = [Buf([("mix", c)]) for c in range(16)]
        dtk = sb("dtk", [128, 4, 16], F32); B_dtk = [Buf([("dtk", j)]) for j in range(4)]
        dAk = sb("dAk", [128, 4, 16], F32); B_dAk = [Buf([("dAk", j)]) for j in range(4)]
        lft = sb("lft", [128, 4, 16], F32); B_lft = [Buf([("lft", j)]) for j in range(4)]
        smallt = sb("smallt", [128, 4, 64], F32); B_small = [Buf([("small", j)]) for j in range(4)]

        UBYTES = 52 * 1024
        U = sb("U", [128, UBYTES // 4], F32)

        def carve(off, nbytes, dt, pattern=None, **kw):
            assert off % 4 == 0 and nbytes % 4 == 0 and off + nbytes <= UBYTES, (off, nbytes)
            ap = U[:, off // 4:(off + nbytes) // 4]
            if dt != F32:
                ap = ap.bitcast(dt)
            if pattern:
                ap = ap.rearrange(pattern, **kw)
            keys = [("U", i) for i in range(off // 1024, (off + nbytes - 1) // 1024 + 1)]
            return ap, Buf(keys)

        KB = 1024
        xTf, B_xTf = carve(0, 16 * KB, F32, "p (k t) -> p k t", k=KC)
        cst = [carve(16 * KB + i * 3 * KB, 2060, F32) for i in range(2)]
        cacc = [carve(22 * KB + i * 2 * KB, 2 * KB, F32) for i in range(2)]
        kf = [(sb("kf%d" % i, [128, 512], F32), Buf([("kf", i)])) for i in range(2)]
        vf = [(sb("vf%d" % i, [128, 512], F32), Buf([("vf", i)])) for i in range(2)]
        vb = [(sb("vb%d" % i, [128, 8, 65], BF16), Buf([("vb", i)])) for i in range(2)]
        stg = [carve(i * 16 * KB, 16 * KB, F32, "p (k c) -> p k c", k=KC) for i in range(2)]
        stgb = [carve(32 * KB + i * 8 * KB, 8 * KB, BF16, "p (k c) -> p k c", k=KC) for i in range(2)]
        Rg2 = [carve(0, 4 * KB, F32, "p (e q) -> p e q", e=8), carve(37 * KB, 4 * KB, F32, "p (e q) -> p e q", e=8)]
        expd2 = [carve(4 * KB, 4 * KB, F32, "p (e q) -> p e q", e=8), carve(41 * KB, 4 * KB, F32, "p (e q) -> p e q", e=8)]
        Eg2 = [carve(8 * KB, 4 * KB, F32, "p (e q) -> p e q", e=8), carve(45 * KB, 4 * KB, F32, "p (e q) -> p e q", e=8)]
        Mb, B_Mb = carve(12 * KB, 4 * KB, BF16, "p (h q) -> p h q", h=16)
        Csb, B_Csb = carve(16 * KB, 4 * KB, BF16, "p (h q) -> p h q", h=16)
        xdt, B_xdt = carve(20 * KB, 2 * KB, BF16)
        xdte, B_xdte = carve(22 * KB, 2 * KB, BF16)
        Btok, B_Btok = carve(24 * KB, 512, BF16)
        CBm, B_CBm = carve(25 * KB, 1 * KB, F32, "p (g q) -> p g q", g=2)
        t1, B_t1 = carve(26 * KB, 4 * KB, F32, "p (c q) -> p c q", c=8)
        gT, B_gT = carve(30 * KB, 4 * KB, F32, "p (c q) -> p c q", c=8)
        gsq, B_gsq = carve(34 * KB, 2 * KB, BF16, "p (c q) -> p c q", c=8)
        rstd, B_rstd = carve(36 * KB, 1 * KB, F32, "p (g q) -> p g q", g=2)
        ktS = [carve(i * 8 * KB, 8 * KB, BF16) for i in range(2)]
        vS = [carve(16 * KB + i * 9 * KB, NKB * 130 * 2, BF16, "p (b d) -> p b d", d=130) for i in range(2)]
        assert NKB * 130 * 2 <= 9 * KB
        PTs = [carve(34 * KB + i * KB, KB, BF16) for i in range(4)]
        Bj = [carve(38 * KB + i * 2 * KB, NKB * 16 * 4, F32, "p (b h) -> p b h", h=16) for i in range(4)]
        assert NKB * 64 <= 2 * KB
        rl2 = [carve(46 * KB, 2 * KB, F32), carve(40 * KB, 2 * KB, F32)]
        rlb2 = [carve(48 * KB, 2 * KB, F32), carve(42 * KB, 2 * KB, F32)]
        onb2 = [carve(50 * KB, 2 * KB, F32), carve(44 * KB, 2 * KB, F32)]
        xres = [carve(34 * KB + i * 4 * KB, 4 * KB, F32) for i in range(2)]
        hb = [carve(42 * KB + i * 4 * KB, 4 * KB, F32) for i in range(2)]
        lnst = [carve(50 * KB + i * KB, 256, F32) for i in range(2)]

        PS = [es.enter_context(nc.psum_tensor("ps%d" % i, [128, 1024], F32)) for i in range(4)]
        B_bank = [Buf([("bank", i)]) for i in range(8)]

        def bank(i):
            return PS[i // 2][:, (i % 2) * 512:(i % 2) * 512 + 512]

        def bank_bf(i):
            return bank(i).bitcast(BF16)

        em.op('pool', lambda e: e.memset(tri[:], 1.0), writes=[B_const])
        em.op('pool', lambda e: e.affine_select(out=tri[:], in_=tri[:], pattern=[[1, 128]], compare_op=ALU.is_ge,
                                                 fill=0.0, base=0, channel_multiplier=-1), reads=[B_const], writes=[B_const])
        em.op('pool', lambda e: e.memset(ustr[:], 1.0), writes=[B_const])
        em.op('pool', lambda e: e.affine_select(out=ustr[:], in_=ustr[:], pattern=[[-1, 128]], compare_op=ALU.is_gt,
                                                 fill=0.0, base=0, channel_multiplier=1), reads=[B_const], writes=[B_const])
        em.op('pool', lambda e: e.memset(ones_f[:], 1.0), writes=[B_const])
        em.op('pool', lambda e: e.memset(ones_b[:], 1.0), writes=[B_const])
        em.op('pool', lambda e: e.tensor_copy(out=cmaskb[:], in_=tri[:]), reads=[B_const], writes=[B_const])
        em.op('pool', lambda e: e.tensor_scalar(out=negmask[:], in0=ustr[:], scalar1=-30000.0, scalar2=None, op0=ALU.mult),
              reads=[B_const], writes=[B_const])
        em.op('pool', lambda e: e.memset(identb[:], 1.0), writes=[B_identb])
        em.op('pool', lambda e: e.affine_select(out=identb[:], in_=identb[:], pattern=[[1, 128]], compare_op=ALU.is_equal,
                                                 fill=0.0, base=0, channel_multiplier=-1), reads=[B_identb], writes=[B_identb])
        for i in range(2):
            em.op('pool', lambda e, i=i: e.memset(vb[i][0][:], 1.0), writes=[vb[i][1]])
        em.op('pool', lambda e: e.memset(Fall[:], 0.0), writes=B_Fall)
        em.op('pool', lambda e: e.memset(onesrow[:], 1.0), writes=["onesrow"])
        for h in range(16):
            em.dma("ks%d" % h, kscr[h, 64:66, :].rearrange("p (r c) -> p r c", c=512),
                   onesrow[0:2, :].unsqueeze(1).broadcast_to([2, KMAX // 512, 512]), reads=["onesrow"], writes=[("kscr", h)])
        em.dma("par", convw[:], convw_d[:, :, :], writes=[B_par])
        em.dma("par", convb[:], convb_d[:, :], writes=[B_par])
        em.dma("par", dcol[:], dcol_d[:, :], writes=[B_par])
        em.dma("par", normw[:], normw_d[:, :], writes=[B_par])
        em.dma("par", dtb[:], dtb_d.partition_broadcast(128), writes=[B_par])
        em.dma("par", Abc[:], alog_d.partition_broadcast(128), writes=[B_par])
        em.dma("par", fbb[:], fb_d.partition_broadcast(128), writes=[B_par])
        em.dma("par", lng[:], lng_d.partition_broadcast(128), writes=[B_par])
        em.dma("par", lnb[:], lnb_d.partition_broadcast(128), writes=[B_par])
        em.op('act', lambda e: e.activation(out=Abc[:], in_=Abc[:], func=AF.Exp), reads=[B_par], writes=[B_par])
        em.op('dve', lambda e: e.tensor_scalar(out=Abc[:], in0=Abc[:], scalar1=-1.0, scalar2=None, op0=ALU.mult),
              reads=[B_par], writes=[B_par])

        def _ck(n):
            if KSTAGE <= n:
                em.stopped = True
        cast_engs = ['dve', 'act', 'dve', 'act', 'pool']
        _ck(1)

        def cast(eng, out, in_, reads, writes):
            if eng == 'act':
                em.op('act', lambda e: e.activation(out=out, in_=in_, func=AF.Copy), reads=reads, writes=writes)
            else:
                em.op(eng, lambda e: e.tensor_copy(out=out, in_=in_), reads=reads, writes=writes)

        for p, c0 in enumerate(GROUP_COLS):
            w = 512
            s = p % 2
            sa, sB = stg[s]
            ba, bB = stgb[s]
            em.dma("wl%d" % s, sa[:, :, 0:w], w_in[:, :, c0:c0 + w].rearrange("k p c -> p k c"), writes=[sB])
            for hlf in range(2):
                cast(cast_engs[(2 * p + hlf) % 5], ba[:, 4 * hlf:4 * hlf + 4, 0:w], sa[:, 4 * hlf:4 * hlf + 4, 0:w],
                     [sB], [bB])
            em.dma("ws%d" % s, wbf[p, :, :], ba.rearrange("p k c -> p (k c)"), reads=[bB], writes=["wbf"])
        for p in range(4):
            s = p % 2
            sa, sB = stg[s]
            sv = sa.rearrange("p k c -> p (k c)").rearrange("p (f d) -> p f d", f=4)
            em.dma("wl%d" % s, sv, w_out[4 * p:4 * p + 4, :, :].rearrange("f p d -> p f d"), writes=[sB])
            for hlf in range(2):
                cast(cast_engs[(2 * p + hlf) % 5], wo[:, 4 * p + 2 * hlf:4 * p + 2 * hlf + 2, :],
                     sv[:, 2 * hlf:2 * hlf + 2, :], [sB], [B_wo[p]])
        sa, sB = stg[0]
        em.dma("wl0", sa[:, :, 0:32], wdtf_f[:, :, :].rearrange("k p c -> p k c"), writes=[sB])
        cast('dve', wdtf[:], sa[:, :, 0:32], [sB], [B_wdtf])

        _ck(2)
        fm_ctr = [0]

        def f_update(lf_ap, B_lf, SB, kbi, fe_idx):
            psF = bank(4)
            em.op('pe', lambda e: e.matmul(psF[0:SB, 256:272], lhsT=tri[0:SB, 0:SB], rhs=lf_ap, start=True, stop=True),
                  reads=[B_lf, B_const], writes=[B_bank[4]], sig=False)
            em.op('pe', lambda e: e.matmul(psF[0:128, 272:288], lhsT=ones_f[0:SB, 0:128], rhs=lf_ap, start=True, stop=True),
                  reads=[B_lf, B_const], writes=[B_bank[4]])
            em.op('dve', lambda e: e.tensor_tensor(out=Fall[0:SB, kbi, :], in0=psF[0:SB, 256:272], in1=Fcar[0:SB, :], op=ALU.add),
                  reads=[B_bank[4], B_Fcar], writes=[B_Fall[kbi]])
            em.op('dve', lambda e: e.tensor_tensor(out=Fend[:, fe_idx, :], in0=psF[:, 272:288], in1=Fcar[:, :], op=ALU.add),
                  reads=[B_bank[4], B_Fcar], writes=[B_Fend[fe_idx]])
            em.op('pool', lambda e: e.tensor_copy(out=Fcar[:, :], in_=Fend[:, fe_idx, :]), reads=[B_Fend[fe_idx]], writes=[B_Fcar])

        w_pref = [False]

        def prefetch_x(xT_src, t0, TBK):
            em.dma("xT", xTf[:, :, 0:TBK], xT_src[:, t0:t0 + TBK].rearrange("(k p) t -> p k t", p=128), writes=[B_xTf])
            em.op('dve', lambda e: e.tensor_copy(out=xT[:, 0:4, 0:TBK], in_=xTf[:, 0:4, 0:TBK]), reads=[B_xTf], writes=[B_xT])
            em.op('act', lambda e: e.activation(out=xT[:, 4:8, 0:TBK], in_=xTf[:, 4:8, 0:TBK], func=AF.Copy), reads=[B_xTf], writes=[B_xT])

        def do_block(xT_src, x_src, t0, TBK, SB, past, outs, last_block, first=False, nxt=None):
            y_o, k_o, v_o, lf_o, conv_o = outs
            nsub = TBK // SB
            key0 = past + t0
            kb0 = key0 // 128
            nkeys = key0 + TBK
            nkb = kb0 + nsub

            def koff(kb):
                return kb * 128 if kb < kb0 else kb0 * 128 + (kb - kb0) * SB

            em.op('pool', lambda e: e.tensor_copy(out=Fblk0[:, :], in_=Fcar[:, :]), reads=[B_Fcar], writes=[B_Fblk0])
            if first:
                prefetch_x(xT_src, t0, TBK)

            psd = bank(4)
            pieces = []

            def piece_dtf():
                for j in range(nsub):
                    for kc in range(KC):
                        em.op('pe', lambda e, j=j, kc=kc: e.matmul(psd[0:SB, 32 * j:32 * j + 32], lhsT=xT[:, kc, j * SB:(j + 1) * SB],
                                                                   rhs=wdtf[:, kc, :], start=(kc == 0), stop=(kc == KC - 1)),
                              reads=[B_xT, B_wdtf], writes=[B_bank[4]], sig=(kc == KC - 1))
            pieces.append(piece_dtf)

            def piece_f(j):
                sm = smallt[0:SB, j, :]
                Bs = B_small[j]
                em.op('dve', lambda e, j=j, sm=sm: e.tensor_tensor(out=sm[:, 0:16], in0=psd[0:SB, 32 * j + 16:32 * j + 32],
                                                                  in1=fbb[0:SB, :], op=ALU.add),
                      reads=[B_bank[4], B_par], writes=[Bs])
                em.op('dve', lambda e, j=j, sm=sm: e.tensor_tensor(out=sm[:, 16:32], in0=psd[0:SB, 32 * j:32 * j + 16],
                                                                  in1=dtb[0:SB, :], op=ALU.add),
                      reads=[B_bank[4], B_par], writes=[Bs])
                em.op('act', lambda e, sm=sm: e.activation(out=sm[:, 0:16], in_=sm[:, 0:16], func=AF.Exp, scale=-1.0),
                      reads=[Bs], writes=[Bs])
                em.op('act', lambda e, sm=sm: e.activation(out=sm[:, 16:32], in_=sm[:, 16:32], func=AF.Exp),
                      reads=[Bs], writes=[Bs])
                em.op('act', lambda e, sm=sm: e.activation(out=sm[:, 0:16], in_=sm[:, 0:16], func=AF.Ln, bias=1.0),
                      reads=[Bs], writes=[Bs])
                em.op('act', lambda e, j=j, sm=sm: e.activation(out=dtk[0:SB, j, :], in_=sm[:, 16:32], func=AF.Ln, bias=1.0),
                      reads=[Bs], writes=[B_dtk[j]])
                em.op('dve', lambda e, j=j, sm=sm: e.tensor_scalar(out=lft[0:SB, j, :], in0=sm[:, 0:16], scalar1=-1.0, scalar2=None,
                                                                  op0=ALU.mult), reads=[Bs], writes=[B_lft[j]])
                em.op('dve', lambda e, j=j: e.tensor_tensor(out=dAk[0:SB, j, :], in0=dtk[0:SB, j, :], in1=Abc[0:SB, :], op=ALU.mult),
                      reads=[B_dtk[j], B_par], writes=[B_dAk[j]])
                em.dma("lfo%d" % j, lf_o[t0 + j * SB:t0 + (j + 1) * SB, :], lft[0:SB, j, :], reads=[B_lft[j]])

            def piece_fB(j):
                f_update(lft[0:SB, j, :], B_lft[j], SB, kb0 + j, j)

            def _mk_f(j):
                def run():
                    if j >= 1:
                        piece_fB(j - 1)
                    if j < nsub:
                        piece_f(j)
                return run
            for j in range(nsub + 1):
                pieces.append(_mk_f(j))

            def piece_delta_a():
                for j in range(nsub):
                    em.op('dve', lambda e, j=j: e.tensor_tensor(out=dlt[0:SB, j, :], in0=Fall[0:SB, kb0 + j, :], in1=Fblk0[0:SB, :],
                                                                op=ALU.subtract), reads=[B_Fall[kb0 + j], B_Fblk0], writes=[B_dlt[j]])
                    em.op('dve', lambda e, j=j: e.tensor_copy(out=dhl[0:SB, j, :, 0], in_=dlt[0:SB, j, :]), reads=[B_dlt[j]], writes=[B_dhl[j]])
                    em.op('dve', lambda e, j=j: e.tensor_tensor(out=dhl[0:SB, j, :, 1], in0=dlt[0:SB, j, :], in1=dhl[0:SB, j, :, 0],
                                                                op=ALU.subtract), reads=[B_dlt[j], B_dhl[j]], writes=[B_dhl[j]])
            pieces.append(piece_delta_a)
            def delta_mm(r0, bq):
                psQ = bank(bq)
                hs = list(range(r0, min(16, r0 + 3)))
                for i, h in enumerate(hs):
                    for j in range(nsub):
                        em.op('pe', lambda e, i=i, h=h, j=j: e.matmul(
                            psQ[32 * i:32 * i + 2, j * SB:(j + 1) * SB], lhsT=dhl[0:SB, j, h, :], rhs=identb[0:SB, 0:SB],
                            start=True, stop=True), reads=[B_dhl[j], B_identb], writes=[B_bank[bq]],
                            sig=(i == len(hs) - 1 and j == nsub - 1))

            def delta_cp(r0, bq):
                psQ = bank(bq)
                hs = list(range(r0, min(16, r0 + 3)))
                for i, h in enumerate(hs):
                    em.op('act', lambda e, i=i, h=h: e.activation(out=qA[64:66, h, 0:TBK], in_=psQ[32 * i:32 * i + 2, 0:TBK], func=AF.Copy),
                          reads=[B_bank[bq]], writes=[B_qA[h]])

            rounds = list(range(0, 16, 3))

            def _mk_d(k):
                def run():
                    for t, r0 in enumerate(rounds[2 * k:2 * k + 2]):
                        delta_mm(r0, 7 - t)
                    for t, r0 in enumerate(rounds[2 * k:2 * k + 2]):
                        delta_cp(r0, 7 - t)
                return run
            for k in range((len(rounds) + 1) // 2):
                pieces.append(_mk_d(k))
            _ck(3)
            groups = []
            for g in range(5):
                groups.append((512 * g, 'fm', g))
            for g in range(2):
                groups.append((K0 + 512 * g, 'fm', 7 + g))
            for g in range(2):
                groups.append((V0 + 512 * g, 'v', g))
            for g in range(2):
                groups.append((Q0 + 512 * g, 'fm', 5 + g))
            for g in range(2):
                groups.append((ZA0 + 512 * g, 'fm', 9 + g))
            SSD_GI = 5

            def load_group(gi):
                c0 = groups[gi][0]
                s = gi % 2
                assert GROUP_COLS[gi] == c0
                em.dma("wg%d" % s, wsl[:, s, :, :].rearrange("p k c -> p (k c)"), wbf[gi, :, :], reads=["wbf"],
                       writes=[B_wsl[s]])

            def ssd_gen():
              for j in range(nsub):
                  cols = slice(j * SB, (j + 1) * SB)
                  psX = bank_bf(5)
                  psB = bank_bf(6)
                  psCB = bank(6)
                  for c in range(8):
                      em.op('pe', lambda e, c=c: e.transpose(psX[0:SB, 128 * c:128 * c + 128], xcT[:, c, cols], identb[:]),
                            reads=[B_xcT[c], B_identb], writes=[B_bank[5]], sig=(c == 7))
                  for g in range(2):
                      em.op('pe', lambda e, g=g: e.transpose(psB[0:SB, 128 * g:128 * g + 128], BCT[:, g, cols], identb[:]),
                            reads=[B_BCT[g], B_identb], writes=[B_bank[6]], sig=(g == 1))
                  for g in range(2):
                      em.op('pe', lambda e, g=g: e.matmul(psCB[0:SB, 256 + 128 * g:256 + 128 * g + SB], lhsT=BCT[:, g, cols],
                                                          rhs=BCT[:, 2 + g, cols], start=True, stop=True),
                            reads=[B_BCT[g], B_BCT[2 + g]], writes=[B_bank[6]], sig=(g == 1))
                  em.op('dve', lambda e, j=j: e.tensor_tensor(
                      out=xdt[0:SB, :].rearrange("p (h d) -> p h d", h=16),
                      in0=psX[0:SB, :].rearrange("p (h d) -> p h d", h=16),
                      in1=dtk[0:SB, j, :].unsqueeze(2).broadcast_to([SB, 16, 64]), op=ALU.mult),
                      reads=[B_bank[5], B_dtk[j]], writes=[B_xdt])
                  em.op('dve', lambda e: e.tensor_copy(out=Btok[0:SB, :], in_=psB[0:SB, 0:256]),
                        reads=[B_bank[6]], writes=[B_Btok])
                  em.op('dve', lambda e: e.tensor_tensor(
                      out=CBm[0:SB, :, 0:SB], in0=psCB[0:SB, 256:512].rearrange("p (g q) -> p g q", g=2)[:, :, 0:SB],
                      in1=tri[0:SB, 0:SB].unsqueeze(1).broadcast_to([SB, 2, SB]), op=ALU.mult),
                      reads=[B_bank[6], B_const], writes=[B_CBm])
                  W8 = 8 * SB
                  nmm = (W8 + 511) // 512
                  for g in range(2):
                      em.op('pool', lambda e, g=g, j=j: e.tensor_tensor(
                          out=Rg2[g][0][0:SB, :, 0:SB], in0=dAk[0:SB, j, 8 * g:8 * g + 8].unsqueeze(2).broadcast_to([SB, 8, SB]),
                          in1=tri[0:SB, 0:SB].unsqueeze(1).broadcast_to([SB, 8, SB]), op=ALU.mult),
                          reads=[B_dAk[j], B_const], writes=[Rg2[g][1]])
                  yield
                  for g in range(2):
                      Rga, B_Rga = Rg2[g]
                      exa, B_exa = expd2[g]
                      ega, B_ega = Eg2[g]
                      for m in range(nmm):
                          e0 = m * (512 // SB)
                          ne = min(8, e0 + 512 // SB) - e0
                          wcols = ne * SB
                          bd, be = m, 2 + m
                          em.op('pe', lambda e: e.matmul(
                              bank(bd)[0:SB, 0:wcols], lhsT=ustr[0:SB, 0:SB], rhs=Rga[0:SB, e0:e0 + ne, 0:SB], start=True, stop=True),
                              reads=[B_Rga, B_const], writes=[B_bank[bd]])
                          em.op('pe', lambda e: e.matmul(
                              bank(be)[0:128, 0:wcols], lhsT=ones_f[0:SB, 0:128], rhs=Rga[0:SB, e0:e0 + ne, 0:SB], start=True, stop=True),
                              reads=[B_Rga, B_const], writes=[B_bank[be]])
                      for m in range(nmm):
                          e0 = m * (512 // SB)
                          ne = min(8, e0 + 512 // SB) - e0
                          wcols = ne * SB
                          bd, be = m, 2 + m
                          em.op('act', lambda e: e.activation(
                              out=exa[0:SB, e0:e0 + ne, 0:SB], in_=bank(bd)[0:SB, 0:wcols].rearrange("p (e q) -> p e q", e=ne), func=AF.Exp),
                              reads=[B_bank[bd]], writes=[B_exa])
                          em.op('act', lambda e: e.activation(
                              out=ega[:, e0:e0 + ne, 0:SB], in_=bank(be)[:, 0:wcols].rearrange("p (e q) -> p e q", e=ne), func=AF.Exp),
                              reads=[B_bank[be]], writes=[B_ega])
                  for g in range(2):
                      exa, B_exa = expd2[g]
                      ega, B_ega = Eg2[g]
                      em.op('dve', lambda e, g=g: e.tensor_tensor(
                          out=Mb[0:SB, 8 * g:8 * g + 8, 0:SB], in0=exa[0:SB, :, 0:SB],
                          in1=CBm[0:SB, g, 0:SB].unsqueeze(1).broadcast_to([SB, 8, SB]), op=ALU.mult),
                          reads=[B_exa, B_CBm], writes=[B_Mb])
                      em.op('dve', lambda e, g=g: e.tensor_tensor(
                          out=Csb[:, 8 * g:8 * g + 8, 0:SB], in0=ega[:, :, 0:SB],
                          in1=BCT[:, 2 + g, cols].unsqueeze(1).broadcast_to([128, 8, SB]), op=ALU.mult),
                          reads=[B_ega, B_BCT[2 + g]], writes=[B_Csb])
                      em.op('pool', lambda e, g=g: e.tensor_tensor(
                          out=xdte[0:SB, 512 * g:512 * g + 512].rearrange("p (h d) -> p h d", h=8),
                          in0=xdt[0:SB, 512 * g:512 * g + 512].rearrange("p (h d) -> p h d", h=8),
                          in1=exa[0:SB, :, SB - 1:SB].broadcast_to([SB, 8, 64]), op=ALU.mult),
                          reads=[B_xdt, B_exa], writes=[B_xdte])
                      em.op('pool', lambda e, g=g: e.tensor_tensor(
                          out=hT[:, 512 * g:512 * g + 512].rearrange("p (h d) -> p h d", h=8),
                          in0=hT[:, 512 * g:512 * g + 512].rearrange("p (h d) -> p h d", h=8),
                          in1=ega[:, :, SB - 1:SB].broadcast_to([128, 8, 64]), op=ALU.mult),
                          reads=[B_hT, B_ega], writes=[B_hT])
                  yield
                  psY = PS[0]
                  for h in range(16):
                      c, hp = h // 2, h % 2
                      bi = (c * SB) // 512
                      em.op('pe', lambda e, h=h, c=c, hp=hp: e.matmul(
                          psY[64 * hp:64 * hp + 64, c * SB:(c + 1) * SB], lhsT=xdt[0:SB, 64 * h:64 * h + 64], rhs=Mb[0:SB, h, 0:SB],
                          start=True, stop=False), reads=[B_xdt, B_Mb], writes=[B_bank[bi]], sig=False)
                      em.op('pe', lambda e, h=h, c=c, hp=hp: e.matmul(
                          psY[64 * hp:64 * hp + 64, c * SB:(c + 1) * SB], lhsT=hTb[:, 64 * h:64 * h + 64], rhs=Csb[:, h, 0:SB],
                          start=False, stop=True), reads=[B_hTb, B_Csb], writes=[B_bank[bi]], sig=True)
                  psS = PS[1]
                  for g in range(2):
                      em.op('pe', lambda e, g=g: e.matmul(psS[:, 512 * g:512 * g + 512], lhsT=Btok[0:SB, 128 * g:128 * g + 128],
                                                          rhs=xdte[0:SB, 512 * g:512 * g + 512], start=True, stop=True),
                            reads=[B_Btok, B_xdte], writes=[B_bank[2 + g]])
                  for g in range(2):
                      em.op('dve', lambda e, g=g: e.tensor_tensor(out=hT[:, 512 * g:512 * g + 512], in0=psS[:, 512 * g:512 * g + 512],
                                                                  in1=hT[:, 512 * g:512 * g + 512], op=ALU.add),
                            reads=[B_bank[2 + g], B_hT], writes=[B_hT])
                  em.op('act', lambda e: e.activation(out=hTb[:, :], in_=hT[:, :], func=AF.Copy), reads=[B_hT], writes=[B_hTb])
                  nb = (8 * SB + 511) // 512
                  em.op('pool', lambda e: e.tensor_tensor(out=t1[:, :, 0:SB], in0=xcT[:, :, cols],
                                                          in1=dcol[:, :].unsqueeze(2).broadcast_to([128, 8, SB]), op=ALU.mult),
                        reads=B_xcT + [B_par], writes=[B_t1])
                  em.op('dve', lambda e: e.tensor_tensor(out=t1[:, :, 0:SB], in0=psY[:, 0:8 * SB].rearrange("p (c q) -> p c q", c=8),
                                                         in1=t1[:, :, 0:SB], op=ALU.add),
                        reads=[B_t1] + [B_bank[i] for i in range(nb)], writes=[B_t1])
                  em.op('dve', lambda e: e.tensor_tensor(out=gT[:, :, 0:SB], in0=t1[:, :, 0:SB], in1=zsT[:, :, cols], op=ALU.mult),
                        reads=[B_t1] + B_zsT, writes=[B_gT])
                  em.op('act', lambda e: e.activation(out=gsq[:, :, 0:SB], in_=gT[:, :, 0:SB], func=AF.Square),
                        reads=[B_gT], writes=[B_gsq])
                  yield
                  psR = bank(5)
                  for g2 in range(2):
                      for cc in range(4):
                          em.op('pe', lambda e, g2=g2, cc=cc: e.matmul(psR[:, g2 * SB:(g2 + 1) * SB], lhsT=ones_b[:, :],
                                                                       rhs=gsq[:, 4 * g2 + cc, 0:SB], start=(cc == 0), stop=(cc == 3)),
                                reads=[B_gsq, B_const], writes=[B_bank[5]], sig=(cc == 3))
                  em.op('act', lambda e: e.activation(out=rstd[:, :, 0:SB], in_=psR[:, 0:2 * SB].rearrange("p (g q) -> p g q", g=2),
                                                      func=AF.Sqrt, scale=1.0 / 512.0, bias=RMS_EPS),
                        reads=[B_bank[5]], writes=[B_rstd])
                  em.op('dve', lambda e: e.reciprocal(out=rstd[:, :, 0:SB], in_=rstd[:, :, 0:SB]), reads=[B_rstd], writes=[B_rstd])
                  em.op('dve', lambda e: e.tensor_tensor(
                      out=t1[:, :, 0:SB].rearrange("p (g c) q -> p g c q", g=2),
                      in0=gT[:, :, 0:SB].rearrange("p (g c) q -> p g c q", g=2),
                      in1=rstd[:, :, 0:SB].unsqueeze(2).broadcast_to([128, 2, 4, SB]), op=ALU.mult),
                      reads=[B_gT, B_rstd], writes=[B_t1])
                  em.op('pool', lambda e: e.tensor_tensor(out=mixT[:, 0:8, cols], in0=t1[:, :, 0:SB],
                                                          in1=normw[:, :].unsqueeze(2).broadcast_to([128, 8, SB]), op=ALU.mult),
                        reads=[B_t1, B_par], writes=B_mix[0:8])
                  yield

            ssd_state = {"gen": None}

            slot_ctr = [0]

            def ssd_slot():
                slot_ctr[0] += 1
                if slot_ctr[0] % 2 == 0:
                    ssd_step()

            def ssd_step():
                if ssd_state["gen"] is None:
                    ssd_state["gen"] = ssd_gen()
                try:
                    next(ssd_state["gen"])
                    return True
                except StopIteration:
                    return False

            if not w_pref[0]:
                load_group(0)
            for gi, (c0, kind, gidx) in enumerate(groups):
                _ck(3.0 + 0.01 * gi)
                if gi + 1 < len(groups) and not (gi == 0 and w_pref[0]):
                    load_group(gi + 1)
                if pieces:
                    pieces.pop(0)()
                if gi >= 6 and pieces:
                    pieces.pop(0)()
                s = gi % 2
                if kind == 'v':
                    hh = gidx
                    for j in range(nsub):
                        bi = (4, 7)[j % 2] if gi >= SSD_GI else 2 + (j % 2)
                        psv = bank(bi)
                        for kc in range(KC):
                            em.op('pe', lambda e, j=j, kc=kc, psv=psv, s=s: e.matmul(
                                psv[0:SB, 0:512], lhsT=xT[:, kc, j * SB:(j + 1) * SB], rhs=wsl[:, s, kc, :],
                                start=(kc == 0), stop=(kc == KC - 1)),
                                reads=[B_xT, B_wsl[s]], writes=[B_bank[bi]], sig=(kc == KC - 1))
                        sl = (hh * nsub + j) % 2
                        vfa, B_vf = vf[sl]
                        vba, B_vb = vb[sl]
                        em.op('act', lambda e, psv=psv, vfa=vfa: e.activation(out=vfa[0:SB, :], in_=psv[0:SB, 0:512], func=AF.Copy),
                              reads=[B_bank[bi]], writes=[B_vf])
                        em.op('dve', lambda e, vfa=vfa, vba=vba: e.tensor_copy(
                            out=vba[0:SB, :, 0:64], in_=vfa[0:SB, :].rearrange("p (h d) -> p h d", h=8)),
                            reads=[B_vf], writes=[B_vb])
                        em.dma("vo%d" % sl, v_o[t0 + j * SB:t0 + (j + 1) * SB, hh * 512:(hh + 1) * 512], vfa[0:SB, :], reads=[B_vf])
                        em.dma("vs%d" % sl, vscr[4 * hh:4 * hh + 4, 0:SB, kb0 + j, :].rearrange("c p d -> p c d"),
                               vba[0:SB, :, :].rearrange("p (c t) d -> p c (t d)", t=2), reads=[B_vb], writes=[("vscr", 4 * hh + i) for i in range(4)])
                        if gi >= SSD_GI:
                            ssd_slot()
                    continue
                for cc in range(4):
                    ch = gidx * 4 + cc
                    if gi >= SSD_GI:
                        bi = (4, 7)[fm_ctr[0] % 2]
                        fm_ctr[0] += 1
                    else:
                        bi = fm_ctr[0] % 2
                        fm_ctr[0] += 1
                    ps = bank(bi)
                    for kc in range(KC):
                        em.op('pe', lambda e, kc=kc, ps=ps, s=s, cc=cc: e.matmul(
                            ps[:, 0:TBK], lhsT=wsl[:, s, kc, 128 * cc:128 * cc + 128], rhs=xT[:, kc, 0:TBK],
                            start=(kc == 0), stop=(kc == KC - 1)),
                            reads=[B_xT, B_wsl[s]], writes=[B_bank[bi]], sig=(kc == KC - 1))
                    Bb = B_bank[bi]
                    if ch < 8:
                        c = ch
                        em.op('act', lambda e, ps=ps, c=c: e.activation(out=zsT[:, c, 0:TBK], in_=ps[:, 0:TBK], func=AF.Silu),
                              reads=[Bb], writes=[B_zsT[c]])
                    elif ch < 20:
                        c = ch - 8
                        sl = c % 2
                        csa, B_cs = cst[sl]
                        caa, B_ca = cacc[sl]
                        em.op('act', lambda e, ps=ps, csa=csa: e.activation(out=csa[:, 3:3 + TBK], in_=ps[:, 0:TBK], func=AF.Copy),
                              reads=[Bb], writes=[B_cs])
                        em.op('pool', lambda e, csa=csa, c=c: e.tensor_copy(out=csa[:, 0:3], in_=ccar[:, c, :]),
                              reads=[B_ccar[c]], writes=[B_cs])
                        em.op('pool', lambda e, csa=csa, c=c: e.tensor_copy(out=ccar[:, c, :], in_=csa[:, TBK:TBK + 3]),
                              reads=[B_cs], writes=[B_ccar[c]])
                        if last_block:
                            em.dma("cvo", conv_o[:, c, :], ccar[:, c, :], reads=[B_ccar[c]])
                        em.op('act', lambda e, ps=ps, caa=caa, c=c: e.activation(
                            out=caa[:, 0:TBK], in_=ps[:, 0:TBK], func=AF.Copy, scale=convw[:, c, 3:4]),
                            reads=[Bb, B_par], writes=[B_ca])
                        for tap in range(0, 3):
                            eng = 'dve'
                            em.op(eng, lambda e, csa=csa, caa=caa, c=c, tap=tap: e.scalar_tensor_tensor(
                                out=caa[:, 0:TBK], in0=csa[:, tap:tap + TBK], scalar=convw[:, c, tap:tap + 1], in1=caa[:, 0:TBK],
                                op0=ALU.mult, op1=ALU.add), reads=[B_cs, B_ca, B_par], writes=[B_ca])
                        if c < 8:
                            dst, Bd = xcT[:, c, 0:TBK], B_xcT[c]
                        else:
                            dst, Bd = BCT[:, c - 8, 0:TBK], B_BCT[c - 8]
                        em.op('act', lambda e, caa=caa, c=c, dst=dst: e.activation(out=dst, in_=caa[:, 0:TBK], func=AF.Silu,
                                                                                   bias=convb[:, c:c + 1]),
                              reads=[B_ca, B_par], writes=[Bd])
                    elif ch < 28:
                        c = ch - 20
                        for hp in range(2):
                            em.op('dve', lambda e, ps=ps, c=c, hp=hp: e.tensor_scalar(
                                out=qA[0:64, 2 * c + hp, 0:TBK], in0=ps[64 * hp:64 * hp + 64, 0:TBK], scalar1=0.125,
                                scalar2=None, op0=ALU.mult), reads=[Bb], writes=[B_qA[2 * c + hp]])
                    elif ch < 36:
                        c = ch - 28
                        sl = c % 2
                        kfa, B_kf = kf[sl]
                        em.op('act', lambda e, ps=ps, kfa=kfa: e.activation(out=kfa[:, 0:TBK], in_=ps[:, 0:TBK], func=AF.Copy),
                              reads=[Bb], writes=[B_kf])
                        em.op('dve', lambda e, kfa=kfa, c=c: e.tensor_copy(out=kTc[:, c, 0:TBK], in_=kfa[:, 0:TBK]),
                              reads=[B_kf], writes=[B_kTc[c]])
                        em.dma("ko%d" % sl, k_o[c, :, t0:t0 + TBK], kfa[:, 0:TBK], reads=[B_kf])
                        for hp in range(2):
                            em.dma("ks%d" % (2 * c + hp), kscr[2 * c + hp, 0:64, key0:key0 + TBK], kTc[64 * hp:64 * hp + 64, c, 0:TBK],
                                   reads=[B_kTc[c]], writes=[("kscr", 2 * c + hp)])
                    else:
                        c = ch - 36
                        em.op('act', lambda e, ps=ps, c=c: e.activation(out=zaT[:, c, 0:TBK], in_=ps[:, 0:TBK], func=AF.Silu),
                              reads=[Bb], writes=[B_zaT[c]])
                    if gi >= SSD_GI:
                        ssd_slot()

            while pieces:
                pieces.pop(0)()
            _ck(4)
            w_pref[0] = nxt is not None
            if w_pref[0]:
                load_group(0)
                load_group(1)
            while ssd_step():
                pass
            _ck(5)
            bba, B_bb = Bj[0]
            em.op('dve', lambda e: e.tensor_tensor(
                out=bba[:, 0:nkb, :], in0=Fblk0[:, :].unsqueeze(1).broadcast_to([128, nkb, 16]), in1=Fall[:, 0:nkb, :],
                op=ALU.subtract), reads=[B_Fblk0] + B_Fall[0:nkb], writes=[B_bb])

            def load_k(h):
                s_ = h % 2
                em.dma("kl%d" % s_, ktS[s_][0][0:66, 0:nkeys], kscr[h, :, 0:nkeys], reads=[("kscr", h)], writes=[ktS[s_][1]])

            def load_v(c):
                s_ = c % 2
                em.dma("vl%d" % s_, vS[s_][0][:, 0:nkb, :], vscr[c, :, 0:nkb, :], reads=[("vscr", c)], writes=[vS[s_][1]])

            S_banks = [0, 1, 4, 6, 7]
            items = [(h, kb) for h in range(16) for kb in range(nkb)]
            LOOK = 4

            def geom(kb):
                if kb < kb0:
                    return 128, 0
                return SB, kb - kb0

            def emit_qk(idx):
                h, kb = items[idx]
                kta, B_kt = ktS[h % 2]
                ksz, i0_ = geom(kb)
                c0 = i0_ * SB
                ko = koff(kb)
                sbk = S_banks[idx % 5]
                ps = bank(sbk)
                diag = kb >= kb0
                em.op('pe', lambda e: e.matmul(ps[0:ksz, c0:TBK], lhsT=kta[0:66, ko:ko + ksz], rhs=qA[0:66, h, c0:TBK],
                                               start=True, stop=(not diag)),
                      reads=[B_kt, B_qA[h]], writes=[B_bank[sbk]], sig=(not diag))
                if diag:
                    em.op('pe', lambda e: e.matmul(ps[0:SB, c0:c0 + SB], lhsT=identb[0:SB, 0:SB], rhs=negmask[0:SB, 0:SB],
                                                   start=False, stop=True),
                          reads=[B_identb, B_const], writes=[B_bank[sbk]])
                if kb == nkb - 1 and h + 2 < 16:
                    load_k(h + 2)

            def emit_exp(idx):
                h, kb = items[idx]
                ksz, i0_ = geom(kb)
                c0 = i0_ * SB
                sbk = S_banks[idx % 5]
                ps = bank(sbk)
                pts, B_pt = PTs[idx % 4]
                em.op('act', lambda e: e.activation(out=pts[0:ksz, c0:TBK], in_=ps[0:ksz, c0:TBK], func=AF.Exp,
                                                    bias=bba[0:ksz, kb, h:h + 1]),
                      reads=[B_bank[sbk], B_bb], writes=[B_pt])

            def emit_pv(idx):
                h, kb = items[idx]
                c, hp = h // 2, h % 2
                vsa, B_vs = vS[c % 2]
                ksz, i0_ = geom(kb)
                c0 = i0_ * SB
                pts, B_pt = PTs[idx % 4]
                ob = 2 + (h % 2)
                psO = bank(ob)
                em.op('pe', lambda e: e.matmul(
                    psO[0:65, c0:TBK], lhsT=vsa[0:ksz, kb, 65 * hp:65 * hp + 65], rhs=pts[0:ksz, c0:TBK],
                    start=(kb == 0), stop=(kb == nkb - 1)), reads=[B_pt, B_vs], writes=[B_bank[ob]])
                if kb == nkb - 1:
                    prt = slice(64 * hp, 64 * hp + 64)
                    rl, B_rl = rl2[h % 2]
                    rlb, B_rlb = rlb2[h % 2]
                    onb, B_on = onb2[h % 2]
                    em.op('dve', lambda e: e.reciprocal(out=rl[64:65, 0:TBK], in_=psO[64:65, 0:TBK]), reads=[B_bank[ob]], writes=[B_rl])
                    if hp == 1 and c + 2 < 8:
                        load_v(c + 2)
                    psL = bank(5)

                    def part2a():
                        em.op('pe', lambda e: e.matmul(psL[0:64, 0:TBK], lhsT=ones_f[64:65, 0:64], rhs=rl[64:65, 0:TBK], start=True, stop=True),
                              reads=[B_rl, B_const], writes=[B_bank[5]])

                    def part2b():
                        em.op('dve', lambda e: e.tensor_copy(out=rlb[0:64, 0:TBK], in_=psL[0:64, 0:TBK]),
                              reads=[B_bank[5]], writes=[B_rlb])
                        em.op('dve', lambda e: e.tensor_tensor(out=onb[prt, 0:TBK], in0=psO[0:64, 0:TBK], in1=rlb[0:64, 0:TBK], op=ALU.mult),
                              reads=[B_bank[ob], B_rlb], writes=[B_on])
                        em.op('pool', lambda e: e.tensor_tensor(out=mixT[prt, 8 + c, 0:TBK], in0=onb[prt, 0:TBK],
                                                                in1=zaT[prt, c, 0:TBK], op=ALU.mult),
                              reads=[B_on, B_zaT[c]], writes=[B_mix[8 + c]])
                    deferred.append((idx + DEFER, part2a))
                    deferred.append((idx + DEFER + 2, part2b))

            load_k(0)
            load_k(1)
            load_v(0)
            load_v(1)
            nit = len(items)
            DEFER = max(1, min(8, nkb - 2))
            deferred = []
            for idx in range(min(LOOK, nit)):
                emit_qk(idx)
            for idx in range(nit):
                emit_exp(idx)
                if idx + LOOK < nit:
                    emit_qk(idx + LOOK)
                deferred.sort(key=lambda t: t[0])
                while deferred and deferred[0][0] <= idx:
                    deferred.pop(0)[1]()
                emit_pv(idx)
            while deferred:
                deferred.pop(0)[1]()

            if nxt is not None:
                prefetch_x(*nxt)
            _ck(6)
            def load_xr(j_):
                s_ = j_ % 2
                em.dma("xr%d" % s_, xres[s_][0][0:SB, :], x_src[t0 + j_ * SB:t0 + (j_ + 1) * SB, :], writes=[xres[s_][1]])
            for j_ in range(min(2, nsub)):
                load_xr(j_)
            for j in range(nsub):
                s = j % 2
                xra, B_xr = xres[s]
                hba, B_hb = hb[s]
                lsa, B_ls = lnst[s]
                psOut = PS[2 + s]
                for half in range(2):
                    bi = 4 + 2 * s + half
                    for fc in range(16):
                        em.op('pe', lambda e, fc=fc, half=half, psOut=psOut, j=j: e.matmul(
                            psOut[0:SB, 512 * half:512 * half + 512], lhsT=mixT[:, fc, j * SB:(j + 1) * SB],
                            rhs=wo[:, fc, 512 * half:512 * half + 512], start=(fc == 0), stop=(fc == 15)),
                            reads=[B_mix[fc], B_wo[fc // 4]], writes=[B_bank[bi]], sig=(fc == 15))
                em.op('dve', lambda e, xra=xra, hba=hba, psOut=psOut: e.scalar_tensor_tensor(
                    out=hba[0:SB, :], in0=xra[0:SB, :], scalar=ALPHA, in1=psOut[0:SB, :], op0=ALU.mult, op1=ALU.add),
                    reads=[B_xr, B_bank[4 + 2 * s], B_bank[5 + 2 * s]], writes=[B_hb])
                if j + 2 < nsub:
                    load_xr(j + 2)
                for half in range(2):
                    em.op('dve', lambda e, hba=hba, lsa=lsa, half=half: e.bn_stats(out=lsa[0:SB, 6 * half:6 * half + 6],
                                                                                  in_=hba[0:SB, 512 * half:512 * half + 512]),
                          reads=[B_hb], writes=[B_ls])
                em.op('dve', lambda e, lsa=lsa: e.bn_aggr(out=lsa[0:SB, 12:14], in_=lsa[0:SB, 0:12]), reads=[B_ls], writes=[B_ls])
                em.op('act', lambda e, lsa=lsa: e.activation(out=lsa[0:SB, 14:15], in_=lsa[0:SB, 13:14], func=AF.Sqrt, bias=LN_EPS),
                      reads=[B_ls], writes=[B_ls])
                em.op('dve', lambda e, lsa=lsa: e.reciprocal(out=lsa[0:SB, 14:15], in_=lsa[0:SB, 14:15]), reads=[B_ls], writes=[B_ls])
                em.op('dve', lambda e, hba=hba, lsa=lsa: e.tensor_scalar(
                    out=hba[0:SB, :], in0=hba[0:SB, :], scalar1=lsa[0:SB, 12:13], scalar2=lsa[0:SB, 14:15],
                    op0=ALU.subtract, op1=ALU.mult), reads=[B_hb, B_ls], writes=[B_hb])
                em.op('pool', lambda e, hba=hba: e.tensor_tensor(out=hba[0:SB, :], in0=hba[0:SB, :], in1=lng[0:SB, :], op=ALU.mult),
                      reads=[B_hb, B_par], writes=[B_hb])
                em.op('pool', lambda e, hba=hba: e.tensor_tensor(out=hba[0:SB, :], in0=hba[0:SB, :], in1=lnb[0:SB, :], op=ALU.add),
                      reads=[B_hb, B_par], writes=[B_hb])
                em.dma("yo%d" % s, y_o[t0 + j * SB:t0 + (j + 1) * SB, :], hba[0:SB, :], reads=[B_hb])

        def init_state_zero():
            _ck(2.5)
            em.op('pool', lambda e: e.memset(hT[:], 0.0), writes=[B_hT])
            em.op('pool', lambda e: e.memset(hTb[:], 0.0), writes=[B_hTb])
            em.op('pool', lambda e: e.memset(ccar[:], 0.0), writes=B_ccar)
            em.op('pool', lambda e: e.memset(Fcar[:], 0.0), writes=[B_Fcar])

        for sq in range(NSEQ):
            init_state_zero()
            outs = (y_p[sq], k_p[sq], v_p[sq], lf_p[sq], conv_p[sq])
            nblk = T // TB
            for bi in range(nblk):
                if bi + 1 < nblk:
                    nxt = (xT_p[sq], (bi + 1) * TB, TB)
                elif sq + 1 < NSEQ:
                    nxt = (xT_p[sq + 1], 0, TB)
                else:
                    nxt = (xT_s, 0, TS)
                do_block(xT_p[sq], x_p[sq], bi * TB, TB, 128, 0, outs, bi == nblk - 1, first=(sq == 0 and bi == 0), nxt=nxt)
            em.dma("sso", ssm_p[sq], hT[:, :], reads=[B_hT])

        em.op('pool', lambda e: e.memset(Fcar[:], 0.0), writes=[B_Fcar])
        em.dma("sti_h", hT[:, :], sssm[:, :], writes=[B_hT])
        em.op('act', lambda e: e.activation(out=hTb[:, :], in_=hT[:, :], func=AF.Copy), reads=[B_hT], writes=[B_hTb])
        em.dma("sti_c", ccar[:], sconv[:, :, :], writes=B_ccar)
        for r in range(0, PAST, 512):
            w = min(512, PAST - r)
            em.dma("xT", xTf[:, :, 0:w], ckT[:, :, r:r + w].rearrange("c p t -> p c t"), writes=[B_xTf])
            em.op('dve', lambda e, w=w: e.tensor_copy(out=kTc[:, 0:4, 0:w], in_=xTf[:, 0:4, 0:w]), reads=[B_xTf], writes=B_kTc[0:4])
            em.op('pool', lambda e, w=w: e.tensor_copy(out=kTc[:, 4:8, 0:w], in_=xTf[:, 4:8, 0:w]), reads=[B_xTf], writes=B_kTc[4:8])
            for c in range(8):
                for hp in range(2):
                    em.dma("ks%d" % (2 * c + hp), kscr[2 * c + hp, 0:64, r:r + w], kTc[64 * hp:64 * hp + 64, c, 0:w],
                           reads=[B_kTc[c]], writes=[("kscr", 2 * c + hp)])
        for pb in range(PAST // 128):
            for hh in range(2):
                sl = hh
                vfa, B_vf = vf[sl]
                vba, B_vb = vb[sl]
                em.dma("vo%d" % sl, vfa[:, :], cv[pb * 128:(pb + 1) * 128, hh * 512:(hh + 1) * 512], writes=[B_vf])
                em.op('dve' if hh == 0 else 'pool', lambda e, vfa=vfa, vba=vba: e.tensor_copy(
                    out=vba[:, :, 0:64], in_=vfa[:, :].rearrange("p (h d) -> p h d", h=8)), reads=[B_vf], writes=[B_vb])
                em.dma("vs%d" % sl, vscr[4 * hh:4 * hh + 4, :, pb, :].rearrange("c p d -> p c d"),
                       vba[:, :, :].rearrange("p (c t) d -> p c (t d)", t=2), reads=[B_vb], writes=[("vscr", 4 * hh + i) for i in range(4)])
            j = pb % 4
            em.dma("lfi%d" % j, lft[:, j, :], clf[pb * 128:(pb + 1) * 128, :], writes=[B_lft[j]])
            f_update(lft[:, j, :], B_lft[j], 128, pb, j)
        do_block(xT_s, x_s, 0, TS, TS, PAST, (y_s, k_s, v_s, lf_s, conv_s), True)
        em.dma("sso", ssm_s, hT[:, :], reads=[B_hT])

        with nc.Block() as block:
            em.replay(block)
    return nc


_CACHE = {}


def _get_prog(NSEQ, T, PAST, TS):
    key = (NSEQ, T, PAST, TS)
    if key not in _CACHE:
        _CACHE[key] = build_program(NSEQ, T, PAST, TS)
    return _CACHE[key]


def kernel(x_prompt, x_sample, cache_k, cache_v, cache_logf, state_conv, state_ssm,
           w_in, conv_w, conv_b, dt_bias, a_log, d_skip, ssm_norm_w, f_bias, w_out, ln_g, ln_b):
    f32 = np.float32
    x_prompt = np.asarray(x_prompt, f32); x_sample = np.asarray(x_sample, f32)
    B, T, _ = x_prompt.shape
    DB, TS, _ = x_sample.shape
    PAST = cache_k.shape[2]
    assert B % N_CORES == 0 and DB == N_CORES
    NSEQ = B // N_CORES
    nc = _get_prog(NSEQ, T, PAST, TS)

    w_in0 = np.asarray(w_in, f32)[0]
    shared = {
        "w_in": np.ascontiguousarray(w_in0.reshape(KC, 128, DIN)),
        "w_out": np.ascontiguousarray(np.asarray(w_out, f32)[0].reshape(16, 128, D)),
        "wdtf_f": np.ascontiguousarray(np.concatenate([w_in0[:, DT0:DT0 + 16], w_in0[:, F0:F0 + 16]], axis=1).reshape(KC, 128, 32)),
        "convw": np.ascontiguousarray(np.asarray(conv_w, f32)[0].T.reshape(12, 128, 4).transpose(1, 0, 2)),
        "convb": np.ascontiguousarray(np.asarray(conv_b, f32)[0].reshape(12, 128).T),
        "dtb": np.ascontiguousarray(np.asarray(dt_bias, f32)[0]),
        "alog": np.ascontiguousarray(np.asarray(a_log, f32)[0]),
        "fb": np.ascontiguousarray(np.asarray(f_bias, f32)[0]),
        "dcol": np.ascontiguousarray(np.repeat(np.asarray(d_skip, f32)[0].reshape(8, 2), 64, axis=1).T),
        "normw": np.ascontiguousarray(np.asarray(ssm_norm_w, f32)[0].reshape(8, 128).T),
        "lng": np.ascontiguousarray(np.asarray(ln_g, f32)[0]),
        "lnb": np.ascontiguousarray(np.asarray(ln_b, f32)[0]),
    }
    ck = np.asarray(cache_k, f32)[0]; cvv = np.asarray(cache_v, f32)[0]; cl = np.asarray(cache_logf, f32)[0]
    sc = np.asarray(state_conv, f32)[0]; ss = np.asarray(state_ssm, f32)[0]
    in_maps = []
    for c in range(N_CORES):
        xp = x_prompt[c * NSEQ:(c + 1) * NSEQ]
        m = dict(shared)
        m["x_p"] = np.ascontiguousarray(xp)
        m["xT_p"] = np.ascontiguousarray(xp.transpose(0, 2, 1))
        m["x_s"] = np.ascontiguousarray(x_sample[c])
        m["xT_s"] = np.ascontiguousarray(x_sample[c].T)
        m["ckT"] = np.ascontiguousarray(ck[c].reshape(PAST, D).T.reshape(8, 128, PAST))
        m["cv"] = np.ascontiguousarray(cvv[c].reshape(PAST, D))
        m["clf"] = np.ascontiguousarray(cl[c])
        m["sconv"] = np.ascontiguousarray(sc[c].T.reshape(12, 128, 3).transpose(1, 0, 2))
        m["sssm"] = np.ascontiguousarray(ss[c].reshape(D, 128).T)
        in_maps.append(m)
    res = run_bass_kernel_spmd(nc, in_maps, core_ids=list(range(N_CORES)))
    R = res.results

    def cat(name):
        return np.concatenate([np.asarray(r[name]) for r in R], axis=0)

    def stack(name):
        return np.stack([np.asarray(r[name]) for r in R], axis=0)

    y_prompt = cat("y_p")
    y_sample = stack("y_s")
    k_prompt = cat("k_p").reshape(B, D, T).transpose(0, 2, 1).reshape(1, B, T, 16, 64)
    v_prompt = cat("v_p").reshape(1, B, T, 16, 64)
    logf_prompt = cat("lf_p").reshape(1, B, T, 16)
    conv_prompt = cat("conv_p").transpose(0, 2, 1, 3).reshape(B, 1536, 3).transpose(0, 2, 1).reshape(1, B, 3, 1536)
    ssm_prompt = cat("ssm_p").transpose(0, 2, 1).reshape(1, B, 16, 64, 128)
    k_sample = stack("k_s").reshape(DB, D, TS).transpose(0, 2, 1).reshape(1, DB, TS, 16, 64)
    v_sample = stack("v_s").reshape(1, DB, TS, 16, 64)
    logf_sample = stack("lf_s").reshape(1, DB, TS, 16)
    conv_sample = stack("conv_s").transpose(0, 2, 1, 3).reshape(DB, 1536, 3).transpose(0, 2, 1).reshape(1, DB, 3, 1536)
    ssm_sample = stack("ssm_s").transpose(0, 2, 1).reshape(1, DB, 16, 64, 128)
    outs = (y_prompt, y_sample, k_prompt, v_prompt, logf_prompt, conv_prompt, ssm_prompt,
            k_sample, v_sample, logf_sample, conv_sample, ssm_sample)
    return tuple(np.ascontiguousarray(o, dtype=f32) for o in outs)
```
